# Optimizing a Trainium2 kernel written in Bass

```python
import jax, jax.numpy as jnp
from jax import lax
import numpy as np

D_MODEL = 1024
BATCH = 8
SEQ = 2048
DEPTH = 1

N_META = 16
CHUNK = 128
EPS = 1e-6

M_HEADS = 4
M_DQK = 128
M_DV = 256
M_QK = M_HEADS * M_DQK
M_V = M_HEADS * M_DV
GATE_CAP = 15.0

S_HEADDIM = 64
S_INNER = D_MODEL
S_HEADS = S_INNER // S_HEADDIM
S_GROUPS = 4
S_HPG = S_HEADS // S_GROUPS
S_STATE = 128
S_CONV = 4
S_XBC = S_INNER + 2 * S_GROUPS * S_STATE

D_FF = -(-8 * D_MODEL // (3 * 256)) * 256

IN_SIZES = (M_QK, M_QK, M_V, M_V, M_HEADS, M_HEADS, S_INNER, S_XBC, S_HEADS, D_MODEL, D_MODEL)
IN_WIDTH = sum(IN_SIZES)

kernel_name = 'hybrid_mlstm_ssd_gated_block'


def rmsnorm(x, g):
    xf = x.astype(jnp.float32)
    y = xf * lax.rsqrt(jnp.mean(xf * xf, axis=-1, keepdims=True) + EPS)
    return (y * g.astype(jnp.float32)).astype(x.dtype)


def softcap(a):
    return GATE_CAP * jnp.tanh(a / GATE_CAP)


def _mlstm_state_step(carry, inp):
    C, n, m = carry
    b_tot, m_loc, S_loc, n_loc = inp
    m_new = jnp.maximum(b_tot + m, m_loc)
    a = jnp.exp(b_tot + m - m_new)
    s = jnp.exp(m_loc - m_new)
    C_new = a[..., None, None] * C + s[..., None, None] * S_loc
    n_new = a[..., None] * n + s[..., None] * n_loc
    return (C_new, n_new, m_new), (C, n, m)


def mlstm_chunked(q, k, v, i_pre, f_pre, valid):
    Bsz, T, H, Dk = q.shape
    Dv = v.shape[-1]
    nc = T // CHUNK
    f32 = jnp.float32
    vmask = valid[None, :, None]
    i_log = jnp.where(vmask, softcap(i_pre.astype(f32)), -jnp.inf)
    f_log = jnp.where(vmask, jax.nn.log_sigmoid(softcap(f_pre.astype(f32))), 0.0)
    q = (q * (Dk ** -0.5)).reshape(Bsz, nc, CHUNK, H, Dk)
    k = k.reshape(Bsz, nc, CHUNK, H, Dk)
    v = v.reshape(Bsz, nc, CHUNK, H, Dv)
    it = jnp.moveaxis(i_log.reshape(Bsz, nc, CHUNK, H), 2, -1)
    bt = jnp.cumsum(jnp.moveaxis(f_log.reshape(Bsz, nc, CHUNK, H), 2, -1), axis=-1)
    b_tot = bt[..., -1]
    causal = jnp.tril(jnp.ones((CHUNK, CHUNK), dtype=bool))
    d_log = jnp.where(causal, bt[..., :, None] - bt[..., None, :] + it[..., None, :], -jnp.inf)

    w_end = b_tot[..., None] - bt + it
    m_loc = jnp.max(w_end, axis=-1)
    wgt = jnp.exp(w_end - m_loc[..., None])
    vw = v * jnp.moveaxis(wgt, -1, 2)[..., None]
    S_loc = jnp.einsum('bcshv,bcshk->bchvk', vw, k)
    n_loc = jnp.einsum('bchs,bcshk->bchk', wgt, k)

    init = (jnp.zeros((Bsz, H, Dv, Dk), f32), jnp.zeros((Bsz, H, Dk), f32), jnp.zeros((Bsz, H), f32))
    xs = (jnp.moveaxis(b_tot, 1, 0), jnp.moveaxis(m_loc, 1, 0),
          jnp.moveaxis(S_loc.astype(f32), 1, 0), jnp.moveaxis(n_loc.astype(f32), 1, 0))
    _, (C_prev, n_prev, m_prev) = lax.scan(_mlstm_state_step, init, xs)
    C_prev = jnp.moveaxis(C_prev, 0, 1)
    n_prev = jnp.moveaxis(n_prev, 0, 1)
    m_prev = jnp.moveaxis(m_prev, 0, 1)

    inter_log = bt + m_prev[..., None]
    m_t = jnp.maximum(inter_log, jnp.max(d_log, axis=-1))
    qk = jnp.einsum('bcthk,bcshk->bchts', q, k)
    w_ts = jnp.exp(d_log - m_t[..., None]) * qk
    inter = jnp.exp(inter_log - m_t)
    num = (jnp.einsum('bchts,bcshv->bcthv', w_ts, v)
           + jnp.einsum('bchvk,bcthk->bcthv', C_prev, q) * jnp.moveaxis(inter, -1, 2)[..., None])
    den = jnp.sum(w_ts, axis=-1) + inter * jnp.einsum('bchk,bcthk->bcht', n_prev, q)
    denom = jnp.maximum(jnp.abs(den), jnp.exp(-m_t))
    h = num / jnp.moveaxis(denom, -1, 2)[..., None]
    return h.reshape(Bsz, T, H, Dv).astype(v.dtype)


def _ssd_state_step(S, inp):
    decay, st = inp
    return decay[..., None, None] * S + st, S


def ssd_chunked(x, dt, A, Bm, Cm):
    Bsz, T, G, J, P = x.shape
    N = Bm.shape[-1]
    nc = T // CHUNK
    x = x.reshape(Bsz, nc, CHUNK, G, J, P)
    dt = dt.reshape(Bsz, nc, CHUNK, G, J)
    Bm = Bm.reshape(Bsz, nc, CHUNK, G, N)
    Cm = Cm.reshape(Bsz, nc, CHUNK, G, N)
    cum = jnp.cumsum(dt * A, axis=2)
    ct = jnp.moveaxis(cum, 2, -1)
    dtT = jnp.moveaxis(dt, 2, -1)
    causal = jnp.tril(jnp.ones((CHUNK, CHUNK), dtype=bool))
    decay = jnp.exp(jnp.where(causal, ct[..., :, None] - ct[..., None, :], -jnp.inf))
    cb = jnp.einsum('bctgn,bcsgn->bcgts', Cm, Bm)
    w = cb[:, :, :, None] * decay * dtT[..., None, :]
    y_diag = jnp.einsum('bcgjts,bcsgjp->bctgjp', w, x)

    end_w = jnp.exp(ct[..., -1:] - ct) * dtT
    xw = x * jnp.moveaxis(end_w, -1, 2)[..., None]
    states = jnp.einsum('bcsgn,bcsgjp->bcgjpn', Bm, xw)
    chunk_decay = jnp.exp(ct[..., -1])
    init = jnp.zeros((Bsz, G, J, P, N), jnp.float32)
    _, S_prev = lax.scan(_ssd_state_step, init,
                         (jnp.moveaxis(chunk_decay, 1, 0), jnp.moveaxis(states.astype(jnp.float32), 1, 0)))
    S_prev = jnp.moveaxis(S_prev, 0, 1)
    y_off = jnp.einsum('bctgn,bcgjpn->bctgjp', Cm, S_prev) * jnp.exp(cum)[..., None]
    return (y_diag + y_off).reshape(Bsz, T, G, J, P)


def setup_inputs(seed: int = 0) -> dict:
    key = jax.random.key(seed)
    ks = jax.random.split(key, 24)
    f32 = jnp.float32
    nrm = lambda k, shape, scale: jax.random.normal(k, shape, f32) * scale
    dt0 = jnp.exp(jax.random.uniform(ks[10], (DEPTH, S_HEADS), f32) * (np.log(0.1) - np.log(1e-3)) + np.log(1e-3))
    return {
        'x': nrm(ks[0], (BATCH, SEQ, D_MODEL), 1.0),
        'meta': nrm(ks[1], (N_META, D_MODEL), 1.0),
        'norm1_g': 1.0 + nrm(ks[2], (DEPTH, D_MODEL), 0.05),
        'w_in': nrm(ks[3], (DEPTH, D_MODEL, IN_WIDTH), D_MODEL ** -0.5),
        'm_igate_b': nrm(ks[4], (DEPTH, M_HEADS), 0.1),
        'm_fgate_b': jnp.linspace(3.0, 6.0, M_HEADS, dtype=f32)[None] + nrm(ks[5], (DEPTH, M_HEADS), 0.1),
        'm_norm_g': 1.0 + nrm(ks[6], (DEPTH, M_HEADS, M_DV), 0.05),
        'm_proj': nrm(ks[7], (DEPTH, M_V, D_MODEL), M_V ** -0.5),
        's_conv_w': nrm(ks[8], (DEPTH, S_CONV, S_XBC), S_CONV ** -0.5),
        's_conv_b': nrm(ks[9], (DEPTH, S_XBC), 0.02),
        's_dt_bias': dt0 + jnp.log(-jnp.expm1(-dt0)),
        's_A_log': jnp.log(jax.random.uniform(ks[11], (DEPTH, S_HEADS), f32, 1.0, 16.0)),
        's_D': 1.0 + nrm(ks[12], (DEPTH, S_HEADS), 0.1),
        's_norm_g': 1.0 + nrm(ks[13], (DEPTH, S_INNER), 0.05),
        's_proj': nrm(ks[14], (DEPTH, S_INNER, D_MODEL), S_INNER ** -0.5),
        'w_out': nrm(ks[15], (DEPTH, D_MODEL, D_MODEL), D_MODEL ** -0.5),
        'norm2_g': 1.0 + nrm(ks[16], (DEPTH, D_MODEL), 0.05),
        'w_ffn_in': nrm(ks[17], (DEPTH, D_MODEL, 2 * D_FF), D_MODEL ** -0.5),
        'w_ffn_out': nrm(ks[18], (DEPTH, D_FF, D_MODEL), D_FF ** -0.5),
        'norm_f_g': 1.0 + nrm(ks[19], (D_MODEL,), 0.05),
    }


def reference(x, meta, norm1_g, w_in, m_igate_b, m_fgate_b, m_norm_g, m_proj, s_conv_w, s_conv_b,
              s_dt_bias, s_A_log, s_D, s_norm_g, s_proj, w_out, norm2_g, w_ffn_in, w_ffn_out, norm_f_g):
    Bsz, S, Dm = x.shape
    n_pad = CHUNK - N_META
    T = S + CHUNK
    h = jnp.concatenate([jnp.zeros((Bsz, n_pad, Dm), x.dtype),
                         jnp.broadcast_to(meta[None].astype(x.dtype), (Bsz, N_META, Dm)),
                         x], axis=1)
    valid = jnp.arange(T) >= n_pad
    split_at = tuple(int(i) for i in np.cumsum(IN_SIZES)[:-1])

    for l in range(DEPTH):
        u = rmsnorm(h, norm1_g[l])
        proj = u @ w_in[l]
        q, k, v, o_pre, i_pre, f_pre, z, xbc, dt_raw, ga, gb = jnp.split(proj, split_at, axis=-1)

        hm = mlstm_chunked(q.reshape(Bsz, T, M_HEADS, M_DQK), k.reshape(Bsz, T, M_HEADS, M_DQK),
                           v.reshape(Bsz, T, M_HEADS, M_DV), i_pre + m_igate_b[l], f_pre + m_fgate_b[l], valid)
        hm = rmsnorm(hm, m_norm_g[l]).reshape(Bsz, T, M_V) * jax.nn.sigmoid(o_pre)
        branch_a = hm @ m_proj[l]

        xbc = xbc * valid[None, :, None].astype(xbc.dtype)
        xpad = jnp.pad(xbc, ((0, 0), (S_CONV - 1, 0), (0, 0)))
        conv = s_conv_b[l] + sum(xpad[:, j:j + T] * s_conv_w[l, j] for j in range(S_CONV))
        xbc = jax.nn.silu(conv)
        xs, Bm, Cm = jnp.split(xbc, (S_INNER, S_INNER + S_GROUPS * S_STATE), axis=-1)
        xs = xs.reshape(Bsz, T, S_GROUPS, S_HPG, S_HEADDIM)
        dt = jax.nn.softplus(dt_raw.astype(jnp.float32) + s_dt_bias[l].astype(jnp.float32))
        dt = jnp.where(valid[None, :, None], dt, 0.0).reshape(Bsz, T, S_GROUPS, S_HPG)
        A = -jnp.exp(s_A_log[l].astype(jnp.float32)).reshape(S_GROUPS, S_HPG)
        ys = ssd_chunked(xs, dt, A, Bm.reshape(Bsz, T, S_GROUPS, S_STATE), Cm.reshape(Bsz, T, S_GROUPS, S_STATE))
        ys = ys + s_D[l].reshape(S_GROUPS, S_HPG)[..., None] * xs
        ys = (ys.reshape(Bsz, T, S_INNER) * jax.nn.silu(z)).astype(x.dtype)
        ys = rmsnorm(ys.reshape(Bsz, T, S_GROUPS, S_INNER // S_GROUPS),
                     s_norm_g[l].reshape(S_GROUPS, S_INNER // S_GROUPS)).reshape(Bsz, T, S_INNER)
        branch_b = ys @ s_proj[l]

        merged = jax.nn.sigmoid(ga) * branch_a + jax.nn.sigmoid(gb) * branch_b
        h = h + (merged @ w_out[l]).astype(h.dtype)

        u2 = rmsnorm(h, norm2_g[l])
        g_in, up = jnp.split(u2 @ w_ffn_in[l], 2, axis=-1)
        h = h + ((jax.nn.silu(g_in) * up) @ w_ffn_out[l]).astype(h.dtype)

    out = rmsnorm(h, norm_f_g)[:, CHUNK:]
    return out.astype(x.dtype)
```

```python
import numpy as np
from contextlib import ExitStack
import concourse.bass as bass
import concourse.mybir as mybir
from concourse.bass_utils import run_bass_kernel_spmd

F32 = mybir.dt.float32
BF16 = mybir.dt.bfloat16
AF = mybir.ActivationFunctionType
ALU = mybir.AluOpType
AX = mybir.AxisListType

NCH = 17
T = NCH * 128
EPS = 1e-6
NEG = -30000.0
OQ, OK_, OV, OO, OI, OF, OZ, OXBC, ODT, OGA, OGB = 0, 512, 1024, 2048, 3072, 3076, 3080, 4104, 6152, 6168, 7192
FF = 2816


_NC = {}


class Dom:
    def __init__(self, name, sem):
        self.name, self.sem, self.count, self.snaps = name, sem, 0, {}


class Trk:
    def __init__(self, nc, es):
        self.nc, self.es = nc, es
        self.eng = {"pe": nc.tensor, "dve": nc.vector, "act": nc.scalar, "pool": nc.gpsimd, "sp": nc.sync}
        self.dom = {k: Dom(k, es.enter_context(nc.semaphore("s_" + k))) for k in self.eng}
        self.known = {k: {} for k in self.eng}
        self.lastw, self.readers = {}, {}
        self.slots = {}
        self.nslot = 0
        self.log = {k: [] for k in self.eng}

    def _need(self, e, tok):
        d, v = tok
        if self.known[e].get(d.name, 0) >= v:
            return
        if e == "pe" and d.name == "pe":
            return
        self.eng[e].wait_ge(d.sem, v)
        self.log[e].append(("w", d.name, v))
        self.known[e][d.name] = v
        snap = d.snaps.get(v)
        if snap:
            for k2, v2 in snap.items():
                if self.known[e].get(k2, 0) < v2:
                    self.known[e][k2] = v2

    def _deps(self, e, r, w):
        toks = {}
        def add(t):
            if t is None:
                return
            d, v = t
            if d.name not in toks or toks[d.name][1] < v:
                toks[d.name] = t
        for k in r:
            add(self.lastw.get(k))
        for k in w:
            add(self.lastw.get(k))
            for t in self.readers.get(k, ()):
                add(t)
        for t in toks.values():
            self._need(e, t)

    def _commit(self, tok, r, w):
        for k in r:
            self.readers.setdefault(k, []).append(tok)
        for k in w:
            self.lastw[k] = tok
            self.readers[k] = []

    def op(self, e, emit, r=(), w=(), inc=True):
        self._deps(e, r, w)
        inst = emit(self.eng[e])
        d = self.dom[e]
        if inc:
            d.count += 1
            inst.then_inc(d.sem, 1)
            self.log[e].append(("i", d.name, 1))
            tok = (d, d.count)
            self.known[e][d.name] = max(self.known[e].get(d.name, 0), 0)
            d.snaps[d.count] = dict(self.known[e])
        else:
            tok = (d, d.count + 1)
        self._commit(tok, r, w)
        return tok

    def dma(self, e, slot, out, in_, r=(), w=()):
        if slot not in self.slots:
            self.slots[slot] = Dom("dma_" + slot, self.es.enter_context(self.nc.semaphore("d_" + slot)))
        d = self.slots[slot]
        if d.count:
            self._need(e, (d, d.count))
        self._deps(e, r, w)
        inst = self.eng[e].dma_start(out=out, in_=in_)
        d.count += 16
        inst.then_inc(d.sem, 16)
        self.log[e].append(("i", d.name, 16))
        tok = (d, d.count)
        self._commit(tok, r, w)
        return tok

    def check(self):
        sem = {}
        pc = {e: 0 for e in self.log}
        prog = True
        while prog:
            prog = False
            for e, lg in self.log.items():
                while pc[e] < len(lg):
                    k, dn, v = lg[pc[e]]
                    if k == "w":
                        if sem.get(dn, 0) < v:
                            break
                    else:
                        sem[dn] = sem.get(dn, 0) + v
                    pc[e] += 1
                    prog = True
        stuck = {e: (pc[e], len(lg), lg[pc[e]], sem.get(lg[pc[e]][1], 0)) for e, lg in self.log.items() if pc[e] < len(lg)}
        return stuck

    def barrier(self):
        for e in self.eng:
            for d in list(self.dom.values()) + list(self.slots.values()):
                if d.count:
                    self._need(e, (d, d.count))


class _Stop(Exception):
    pass


def build(debug=False, stop=None):
    nc = bass.Bass("TRN2", target_bir_lowering=False)
    dr = lambda n, s, k="ExternalInput": nc.dram_tensor(n, s, F32, kind=k).ap()
    x = dr("x", [2048, 1024]); h0 = dr("h0", [128, 1024])
    w_in = dr("w_in", [1024, 8216]); m_proj = dr("m_proj", [1024, 1024]); s_proj = dr("s_proj", [1024, 1024])
    w_out = dr("w_out", [1024, 1024]); w_ffn_in = dr("w_ffn_in", [1024, 2 * FF]); w_ffn_out = dr("w_ffn_out", [FF, 1024])
    norm1_g = dr("norm1_g", [1024]); norm2_g = dr("norm2_g", [1024]); norm_f_g = dr("norm_f_g", [1024])
    m_norm_g = dr("m_norm_g", [1024]); s_norm_g = dr("s_norm_g", [1024])
    gate_b = dr("gate_b", [8]); dt_bias = dr("dt_bias", [16]); a_log = dr("a_log", [16]); s_d = dr("s_d", [16])
    convw = dr("convw", [2048, 4]); convb = dr("convb", [1, 2048])
    c_ident = dr("c_ident", [128, 128]); c_tri = dr("c_tri", [128, 128]); c_negm = dr("c_negm", [128, 512])
    c_sel = dr("c_sel", [96, 2048]); c_valid = dr("c_valid", [128, 1])
    out = dr("out", [2048, 1024], "ExternalOutput")
    dbg = {}
    if debug:
        for n, s in (("d_uT", [128, 8 * T]), ("d_hmT", [128, 8 * 2048]), ("d_ysT", [128, 8 * 2048]),
                     ("d_mgT", [128, 8 * 2048]), ("d_h2", [128, 16 * 1024]), ("d_g", [128, 17 * 4 * 8])):
            dbg[n] = dr(n, s, "ExternalOutput")

    es = ExitStack()
    try:
      with es:
        tk = Trk(nc, es)
        _NC["trk"] = tk
        def maybe_stop(ph):
            if stop == ph:
                tk.barrier()
                raise _Stop()
        op, dma = tk.op, tk.dma

        def sb(stack, name, shape, dt=F32):
            return stack.enter_context(nc.sbuf_tensor(name, shape, dt))

        class Rot:
            def __init__(self, stack, name, shape, dt, n, hold=False):
                self.t = [sb(stack, f"{name}{i}", shape, dt) for i in range(n)]
                self.name, self.i, self.hold, self.held = name, 0, hold, set()
            def next(self):
                i = self.i % len(self.t); self.i += 1
                if self.hold:
                    assert i not in self.held, f"rotating buffer {self.name} reused while still held"
                    self.held.add(i)
                return self.t[i], (self.name, i)
            def release(self, key):
                self.held.discard(key[1])

        PB = [es.enter_context(nc.psum_tensor(f"pb{i}", [128, 512], F32)) for i in range(6)]
        PTB = [es.enter_context(nc.psum_tensor(f"ptb{i}", [128, 1024], BF16)) for i in range(2)]
        st = {"pb": 0, "ptb": 0}
        def pbank():
            i = st["pb"] % 6; st["pb"] += 1
            return PB[i], ("pb", i)
        def ptbank():
            i = st["ptb"] % 2; st["ptb"] += 1
            return PTB[i], ("ptb", i)

        identf = sb(es, "identf", [128, 128]); identb = sb(es, "identb", [128, 128], BF16)
        tri = sb(es, "tri", [128, 128]); ones = sb(es, "ones", [128, 128])
        negm = sb(es, "negm", [128, 512], BF16); sel = sb(es, "sel", [96, 2048], BF16); valid = sb(es, "valid", [128, 1])
        dma("sp", "c0", identf[:], c_ident, w=["identf"])
        dma("sp", "c1", tri[:], c_tri, w=["tri"])
        dma("pool", "g2", sel[:], c_sel, w=["sel"])
        dma("sp", "c3", valid[:], c_valid, w=["valid"])
        dma("pool", "g0", identb[:], c_ident, w=["identb"])
        dma("pool", "g1", negm[:], c_negm, w=["negm"])
        op("dve", lambda e: e.memset(ones[:], 1.0), w=["ones"])

        arena = sb(es, "arena", [128, 49152], BF16)

        def aview(off, shape, dt=BF16, parts=128):
            n = 1
            for d_ in shape[1:]:
                n *= d_
            nb = n * (4 if dt == F32 else 2)
            assert off % 4 == 0 and off + nb <= 98304
            v = arena[0:parts, off // 2:(off + nb) // 2]
            if dt == F32:
                v = v.bitcast(F32)
            if len(shape) == 3:
                v = v.rearrange("p (a b) -> p a b", a=shape[1])
            return v

        class ARot:
            def __init__(self, name, off, shape, dt, n):
                sz = 1
                for d_ in shape[1:]:
                    sz *= d_
                sz = (sz * (4 if dt == F32 else 2) + 31) // 32 * 32
                self.t = [aview(off + i * sz, shape, dt) for i in range(n)]
                self.name, self.i, self.end = name, 0, off + n * sz
            def next(self):
                i = self.i % len(self.t); self.i += 1
                return self.t[i], (self.name, i)

        R0, R1, R2 = 0, 32768, 65536
        hmT = aview(R0, [128, 8, 2048]); ysT = aview(R1, [128, 8, 2048]); mgT = aview(R2, [128, 8, 2048])
        h2 = aview(R0, [128, 16, 1024], F32); u2T = aview(R2, [128, 8, 2048])
        su = ExitStack()
        su.__enter__()
        uT = sb(su, "uT", [128, 8, T], BF16)

        def load_bc(stack, name, src, n, slot):
            t = sb(stack, name, [128, n])
            dma("sp", slot, t[:], src.partition_broadcast(128), w=[name])
            return t

        def rms_rstd(stack_rot, src_ap, src_keys, n, scale_ap=None, eps=EPS):
            junk, kj = stack_rot["junk"].next()
            ssq, ks = stack_rot["ssq"].next()
            rs, kr = stack_rot["rstd"].next()
            if scale_ap is None:
                op("act", lambda e: e.activation(out=junk[:, 0:n], in_=src_ap, func=AF.Square, accum_out=ssq[:]),
                   r=src_keys, w=[kj, ks])
            else:
                sa, sk = scale_ap
                op("act", lambda e: e.activation(out=junk[:, 0:n], in_=src_ap, func=AF.Square, scale=sa, accum_out=ssq[:]),
                   r=list(src_keys) + [sk], w=[kj, ks])
            op("dve", lambda e: e.tensor_scalar(out=rs[:], in0=ssq[:], scalar1=1.0 / n, scalar2=eps, op0=ALU.mult, op1=ALU.add),
               r=[ks], w=[kr])
            op("act", lambda e: e.activation(out=rs[:], in_=rs[:], func=AF.Sqrt), r=[kr], w=[kr])
            op("dve", lambda e: e.reciprocal(out=rs[:], in_=rs[:]), r=[kr], w=[kr])
            return rs, kr

        def norm_to_T(rots, src_ap, src_keys, g_bc, g_key, dstT, c_dst, dkeys):
            rs, kr = rms_rstd(rots, src_ap, src_keys, 1024)
            yield
            ub, ku = rots["ub"].next()
            op("dve", lambda e: e.scalar_tensor_tensor(out=ub[:], in0=src_ap, scalar=rs[:, 0:1], in1=g_bc[:],
                                                      op0=ALU.mult, op1=ALU.mult), r=list(src_keys) + [kr, g_key], w=[ku])
            yield
            pt, kp = ptbank()
            for kc in range(8):
                op("pe", lambda e: e.transpose(out=pt[:, kc * 128:(kc + 1) * 128], in_=ub[:, kc * 128:(kc + 1) * 128],
                                               identity=identb[:]), r=[ku, "identb"], w=[kp], inc=(kc == 7))
            op("act", lambda e: e.copy(out=dstT[:, :, c_dst * 128:(c_dst + 1) * 128],
                                       in_=pt[:].rearrange("p (k t) -> p k t", k=8)), r=[kp], w=dkeys)

        def run_jobs(jobs, depth, limits=None):
            active, done, nxt, bg = [], set(), 0, []
            limits = limits or {}
            while nxt < len(jobs) or active or bg:
                started = 0
                while nxt < len(jobs) and started < 1:
                    fn, deps = jobs[nxt][0], jobs[nxt][1]
                    cls = jobs[nxt][2] if len(jobs[nxt]) > 2 else "chunk"
                    if deps == "bg":
                        bg.append((nxt, fn(), "bg"))
                        nxt += 1
                        continue
                    ncls = sum(1 for it in active if it[2] == cls)
                    if len(active) >= depth or ncls >= limits.get(cls, depth):
                        break
                    if deps == "drain":
                        if active:
                            break
                    elif not all(d_ in done for d_ in deps):
                        break
                    active.append((nxt, fn(), cls))
                    nxt += 1
                    started += 1
                assert active or bg
                for lst in (active, bg):
                    for item in list(lst):
                        try:
                            next(item[1])
                        except StopIteration:
                            lst.remove(item)
                            done.add(item[0])

        esA = ExitStack()
        with esA:
            g1 = load_bc(esA, "g1bc", norm1_g, 1024, "c0")
            rots = {"junk": Rot(esA, "junk", [128, 1024], BF16, 2), "ssq": Rot(esA, "ssq", [128, 1], F32, 8),
                    "rstd": Rot(esA, "rstd", [128, 1], F32, 8), "ub": Rot(esA, "ub", [128, 1024], BF16, 6)}
            xin = Rot(esA, "xin", [128, 1024], F32, 6)

            def a_job(c):
                def gen():
                    xt, kx = xin.next()
                    dma("sp", f"x{c % 6}", xt[:], h0 if c == 0 else x[(c - 1) * 128:c * 128, :], w=[kx])
                    yield from norm_to_T(rots, xt[:], [kx], g1, "g1bc", uT, c, [("uT", c)])
                return gen
            run_jobs([(a_job(c), ()) for c in range(NCH)], 6)
            tk.barrier()
        maybe_stop("A")
        if debug:
            dma("pool", "dbgp", dbg["d_uT"], uT[:].rearrange("p k t -> p (k t)"), r=[("uT", c) for c in range(NCH)])

        uT_keys = lambda c0, c1: [("uT", c) for c in range(c0, c1)]
        TILES = [(0, 4), (4, 8), (8, 12), (12, 16), (16, 17)]

        def wload(t_ap, src_ap, key, slot):
            dma("pool", slot, t_ap, src_ap, w=[key])

        def wview(src, c0, n):
            return src[:, c0:c0 + n].rearrange("(kc p) n -> p kc n", p=128)

        gb8 = load_bc(su, "gb8", gate_b, 8, "c1")
        wg = sb(su, "wg", [128, 8, 8], BF16)
        wload(wg[:], wview(w_in, OI, 8), "wg", "g0")
        _go = [R2 + 25600]
        def gtile(shape):
            n_ = 4
            for d_ in shape[1:]:
                n_ *= d_
            v = aview(_go[0], shape, F32)
            _go[0] += (n_ + 31) // 32 * 32
            return v
        pre = gtile([128, NCH, 8]); th = gtile([128, NCH, 8])
        ilog, flog, bt, btot, a_, Ab, g_, m_, ea, sc, dn = [gtile([128, NCH, 4]) for _ in range(11)]
        mm = gtile([128, NCH + 1, 4]); amax = gtile([128, 1]); dg = gtile([128, 68])
        assert _go[0] <= R2 + 32768

        def gates_gen():
                pg, kpg = pbank()
                for c in range(NCH):
                    for kc in range(8):
                        op("pe", lambda e: e.matmul(pg[:, c * 8:(c + 1) * 8], lhsT=uT[:, kc, c * 128:(c + 1) * 128], rhs=wg[:, kc, :],
                                                    start=(kc == 0), stop=(kc == 7)), r=[("uT", c), "wg"], w=[kpg],
                           inc=(c == NCH - 1 and kc == 7))
                op("dve", lambda e: e.tensor_tensor(out=pre[:], in0=pg[:, 0:NCH * 8].rearrange("p (c e) -> p c e", e=8),
                                                    in1=gb8[:].unsqueeze(1).to_broadcast([128, NCH, 8]), op=ALU.add),
                   r=[kpg, "gb8"], w=["pre"])
                yield
                op("act", lambda e: e.activation(out=th[:], in_=pre[:], func=AF.Tanh, scale=1.0 / 15.0), r=["pre"], w=["th"])
                yield
                op("dve", lambda e: e.tensor_scalar(out=ilog[:], in0=th[:, :, 0:4], scalar1=15.0, scalar2=None, op0=ALU.mult),
                   r=["th"], w=["ilog"])
                yield
                op("act", lambda e: e.activation(out=flog[:], in_=th[:, :, 4:8], func=AF.Exp, scale=-15.0), r=["th"], w=["flog"])
                yield
                op("act", lambda e: e.activation(out=flog[:], in_=flog[:], func=AF.Ln, bias=1.0), r=["flog"], w=["flog"])
                yield
                op("dve", lambda e: e.tensor_scalar(out=flog[:], in0=flog[:], scalar1=-1.0, scalar2=None, op0=ALU.mult),
                   r=["flog"], w=["flog"])
                yield
                op("dve", lambda e: e.tensor_scalar(out=flog[:, 0, :], in0=flog[:, 0, :], scalar1=valid[:, 0:1], scalar2=None,
                                                    op0=ALU.mult), r=["flog", "valid"], w=["flog"])
                yield
                fl2 = flog[:].rearrange("p c h -> p (c h)")
                p1, k1 = pbank()
                op("pe", lambda e: e.matmul(p1[:, 0:68], lhsT=tri[:], rhs=fl2, start=True, stop=True), r=["tri", "flog"], w=[k1])
                op("act", lambda e: e.copy(out=bt[:].rearrange("p c h -> p (c h)"), in_=p1[:, 0:68]), r=[k1], w=["bt"])
                yield
                p2, k2 = pbank()
                op("pe", lambda e: e.matmul(p2[:, 0:68], lhsT=ones[:], rhs=fl2, start=True, stop=True), r=["ones", "flog"], w=[k2])
                op("act", lambda e: e.copy(out=btot[:].rearrange("p c h -> p (c h)"), in_=p2[:, 0:68]), r=[k2], w=["btot"])
                yield
                op("dve", lambda e: e.tensor_tensor(out=a_[:], in0=ilog[:], in1=bt[:], op=ALU.subtract), r=["ilog", "bt"], w=["a_"])
                yield
                p3, k3 = pbank()
                op("pe", lambda e: e.transpose(out=p3[0:68, 0:128], in_=a_[:].rearrange("p c h -> p (c h)"), identity=identf[:]),
                   r=["a_", "identf"], w=[k3])
                op("dve", lambda e: e.reduce_max(out=amax[0:68, :], in_=p3[0:68, 0:128], axis=AX.X), r=[k3], w=["amax"])
                yield
                op("dve", lambda e: e.tensor_scalar(out=dg[0:68, :], in0=identf[0:68, 0:68], scalar1=amax[0:68, 0:1], scalar2=None,
                                                    op0=ALU.mult), r=["amax", "identf"], w=["dg"])
                yield
                p4, k4 = pbank()
                op("pe", lambda e: e.matmul(p4[:, 0:68], lhsT=ones[0:68, :], rhs=dg[0:68, :], start=True, stop=True),
                   r=["ones", "dg"], w=[k4])
                op("act", lambda e: e.copy(out=Ab[:].rearrange("p c h -> p (c h)"), in_=p4[:, 0:68]), r=[k4], w=["Ab"])
                yield
                op("dve", lambda e: e.memset(mm[:, 0, :], 0.0), w=["mm"])
                yield
                for c in range(NCH):
                    op("dve", lambda e: e.tensor_tensor(out=g_[:, c, :], in0=mm[:, c, :], in1=Ab[:, c, :], op=ALU.max),
                       r=["mm", "Ab"], w=["g_"])
                    op("dve", lambda e: e.tensor_tensor(out=mm[:, c + 1, :], in0=g_[:, c, :], in1=btot[:, c, :], op=ALU.add),
                       r=["g_", "btot"], w=["mm"])
                    yield
                op("dve", lambda e: e.tensor_tensor(out=ea[:], in0=a_[:], in1=g_[:], op=ALU.subtract), r=["a_", "g_"], w=["ea"])
                yield
                op("act", lambda e: e.activation(out=ea[:], in_=ea[:], func=AF.Exp), r=["ea"], w=["ea"])
                yield
                op("dve", lambda e: e.tensor_tensor(out=sc[:], in0=mm[:, 0:NCH, :], in1=g_[:], op=ALU.subtract), r=["mm", "g_"], w=["sc"])
                yield
                op("act", lambda e: e.activation(out=sc[:], in_=sc[:], func=AF.Exp), r=["sc"], w=["sc"])
                yield
                op("dve", lambda e: e.tensor_tensor(out=dn[:], in0=bt[:], in1=g_[:], op=ALU.add), r=["bt", "g_"], w=["dn"])
                yield
                op("act", lambda e: e.activation(out=dn[:], in_=dn[:], func=AF.Exp, scale=-1.0), r=["dn"], w=["dn"])
                yield


        esC = ExitStack()
        with esC:
            sn = load_bc(esC, "snbc", s_norm_g, 1024, "c0")
            dtb = load_bc(esC, "dtb", dt_bias, 16, "c1")
            alg = load_bc(esC, "alg", a_log, 16, "c2")
            Dbc = load_bc(esC, "Dbc", s_d, 16, "c3")
            cw = sb(esC, "cw", [128, 16, 4]); cbrow = sb(esC, "cbrow", [1, 2048], BF16)
            onesb = sb(esC, "onesb", [1, 512], BF16)
            dma("sp", "c0", cw[:], convw.rearrange("(b p) j -> p b j", p=128), w=["cw"])
            dma("pool", "g1", cbrow[:], convb, w=["cbrow"])
            op("dve", lambda e: e.tensor_scalar(out=cbrow[:], in0=cbrow[:], scalar1=0.5, scalar2=None, op0=ALU.mult), r=["cbrow"], w=["cbrow"])
            op("dve", lambda e: e.memset(onesb[:], 1.0), w=["onesb"])
            wdt = sb(esC, "wdt", [128, 8, 16], BF16)
            wload(wdt[:], wview(w_in, ODT, 16), "wdt", "g0")
            pd, kpd = pbank()
            for c in range(NCH):
                for kc in range(8):
                    op("pe", lambda e: e.matmul(pd[:, c * 16:(c + 1) * 16], lhsT=uT[:, kc, c * 128:(c + 1) * 128], rhs=wdt[:, kc, :],
                                                start=(kc == 0), stop=(kc == 7)), r=[("uT", c), "wdt"], w=[kpd],
                       inc=(c == NCH - 1 and kc == 7))
            esCt = ExitStack(); esCt.__enter__()
            H = lambda nm, st_=None: sb(st_ if st_ is not None else esC, nm, [128, NCH, 16])
            dt_, ecum, chdec, dtw = [H(n) for n in ("dt_", "ecum", "chdec", "dtw")]
            Abc = sb(esC, "Abc", [128, 16]); DIm = aview(R2 + 20736, [128, 16, 128])
            xd, ax, dA, cum, ctot, endw = [H(n, esCt) for n in ("xd", "ax", "dA", "cum", "ctot", "endw")]
            bc16 = lambda t: t[:].unsqueeze(1).to_broadcast([128, NCH, 16])
            op("dve", lambda e: e.tensor_tensor(out=xd[:], in0=pd[:, 0:NCH * 16].rearrange("p (c e) -> p c e", e=16),
                                                in1=bc16(dtb), op=ALU.add), r=[kpd, "dtb"], w=["xd"])
            op("act", lambda e: e.activation(out=ax[:], in_=xd[:], func=AF.Abs), r=["xd"], w=["ax"])
            op("act", lambda e: e.activation(out=ax[:], in_=ax[:], func=AF.Exp, scale=-1.0), r=["ax"], w=["ax"])
            op("act", lambda e: e.activation(out=ax[:], in_=ax[:], func=AF.Ln, bias=1.0), r=["ax"], w=["ax"])
            op("dve", lambda e: e.scalar_tensor_tensor(out=dt_[:], in0=xd[:], scalar=0.0, in1=ax[:], op0=ALU.max, op1=ALU.add),
               r=["xd", "ax"], w=["dt_"])
            op("dve", lambda e: e.tensor_scalar(out=dt_[:, 0, :], in0=dt_[:, 0, :], scalar1=valid[:, 0:1], scalar2=None, op0=ALU.mult),
               r=["dt_", "valid"], w=["dt_"])
            op("act", lambda e: e.activation(out=Abc[:], in_=alg[:], func=AF.Exp), r=["alg"], w=["Abc"])
            op("dve", lambda e: e.scalar_tensor_tensor(out=dA[:], in0=dt_[:], scalar=-1.0, in1=bc16(Abc), op0=ALU.mult, op1=ALU.mult),
               r=["dt_", "Abc"], w=["dA"])
            dA2 = dA[:].rearrange("p c h -> p (c h)")
            p1, k1 = pbank()
            op("pe", lambda e: e.matmul(p1[:, 0:272], lhsT=tri[:], rhs=dA2, start=True, stop=True), r=["tri", "dA"], w=[k1])
            op("act", lambda e: e.copy(out=cum[:].rearrange("p c h -> p (c h)"), in_=p1[:, 0:272]), r=[k1], w=["cum"])
            p2, k2 = pbank()
            op("pe", lambda e: e.matmul(p2[:, 0:272], lhsT=ones[:], rhs=dA2, start=True, stop=True), r=["ones", "dA"], w=[k2])
            op("act", lambda e: e.copy(out=ctot[:].rearrange("p c h -> p (c h)"), in_=p2[:, 0:272]), r=[k2], w=["ctot"])
            op("act", lambda e: e.activation(out=ecum[:], in_=cum[:], func=AF.Exp), r=["cum"], w=["ecum"])
            op("act", lambda e: e.activation(out=chdec[:], in_=ctot[:], func=AF.Exp), r=["ctot"], w=["chdec"])
            op("dve", lambda e: e.tensor_tensor(out=endw[:], in0=ctot[:], in1=cum[:], op=ALU.subtract), r=["ctot", "cum"], w=["endw"])
            op("act", lambda e: e.activation(out=endw[:], in_=endw[:], func=AF.Exp), r=["endw"], w=["endw"])
            op("dve", lambda e: e.tensor_tensor(out=dtw[:], in0=endw[:], in1=dt_[:], op=ALU.mult), r=["endw", "dt_"], w=["dtw"])
            maybe_stop("C00")
            cfm = aview(R0 + 17408, [128, T], BF16); ncfm = aview(R2, [128, T], BF16)
            csp = sb(esCt, "csp", [128, NCH, 128], BF16); r1 = sb(esCt, "r1", [128, NCH, 16]); ncl = sb(esCt, "ncl", [128, NCH, 16])
            op("dve", lambda e: e.tensor_scalar(out=ncl[:], in0=dt_[:], scalar1=1e-30, scalar2=None, op0=ALU.max), r=["dt_"], w=["ncl"])
            op("act", lambda e: e.activation(out=ncl[:], in_=ncl[:], func=AF.Ln), r=["ncl"], w=["ncl"])
            op("dve", lambda e: e.tensor_tensor(out=ncl[:], in0=ncl[:], in1=cum[:], op=ALU.subtract), r=["ncl", "cum"], w=["ncl"])
            for src_, srck, dst_, dstk in ((cum, "cum", cfm, "cfm"), (ncl, "ncl", ncfm, "ncfm")):
                op("pool", lambda e: e.memset(csp[:], 0.0), w=["csp"])
                op("dve", lambda e: e.tensor_copy(out=csp[:, :, 0:16], in_=src_[:]), r=[srck], w=["csp"])
                op("dve", lambda e: e.tensor_tensor(out=r1[:], in0=src_[:], in1=csp[:, :, 0:16], op=ALU.subtract), r=[srck, "csp"], w=["r1"])
                op("dve", lambda e: e.tensor_copy(out=csp[:, :, 32:48], in_=r1[:]), r=["r1"], w=["csp"])
                op("dve", lambda e: e.tensor_tensor(out=r1[:], in0=r1[:], in1=csp[:, :, 32:48], op=ALU.subtract), r=["r1", "csp"], w=["r1"])
                op("dve", lambda e: e.tensor_copy(out=csp[:, :, 64:80], in_=r1[:]), r=["r1"], w=["csp"])
                for c0 in range(0, NCH, 8):
                    c1 = min(NCH, c0 + 8)
                    pp, kp = ptbank()
                    for c in range(c0, c1):
                        op("pe", lambda e: e.transpose(out=pp[:, (c - c0) * 128:(c - c0 + 1) * 128], in_=csp[:, c, :], identity=identb[:]),
                           r=["csp", "identb"], w=[kp], inc=(c == c1 - 1))
                    n = (c1 - c0) * 128
                    op("act", lambda e: e.copy(out=dst_[:, c0 * 128:c1 * 128], in_=pp[:, 0:n]), r=[kp], w=[dstk])
            maybe_stop("C01")
            op("dve", lambda e: e.tensor_scalar(out=cw[:], in0=cw[:], scalar1=0.5, scalar2=None, op0=ALU.mult), r=["cw"], w=["cw"])
            for hd in range(16):
                op("dve", lambda e: e.tensor_scalar(out=DIm[:, hd, :], in0=identf[:], scalar1=Dbc[:, hd:hd + 1], scalar2=None, op0=ALU.mult),
                   r=["identf", "Dbc"], w=[("DIm", hd)])
            tk.barrier()
            esCt.__exit__(None, None, None)
            maybe_stop("C0")

            wsf = ARot("wsf", R2 + 4352, [128, 8, 512], BF16, 2)
            wsz = Rot(esC, "wsz", [128, 8, 256], BF16, 2)
            raw = ARot("raw", R0 + 17408 + 4352, [128, 4 + 512], BF16, 2)
            cth = Rot(esC, "cth", [128, 512], BF16, 2); dgms = ARot("dgms", R0 + 24576, [128, 16, 128], BF16, 2)
            fT = aview(R0, [128, 4, T])
            Sb = Rot(esC, "Sb", [128, 256], BF16, 4)
            xtk = Rot(esC, "xtk", [128, 256], BF16, 4); xw = Rot(esC, "xw", [128, 256], BF16, 4)
            btk = Rot(esC, "btk", [128, 128], BF16, 4); cbs = Rot(esC, "cbs", [128, 128], F32, 4)
            dec = Rot(esC, "dec", [128, 512], F32, 4); w4 = Rot(esC, "w4", [128, 512], BF16, 4)
            sz = Rot(esC, "sz", [128, 256], F32, 4); yt = Rot(esC, "yt", [128, 256], F32, 12, hold=True); ysb = Rot(esC, "ysb", [128, 256], BF16, 8)
            rotsC = {"junk": Rot(esC, "junkC", [128, 256], BF16, 2)}

            def load_group(g):
                wf, kf = wsf.next(); wz, kz = wsz.next()
                wload(wf[:, :, 0:256], wview(w_in, OXBC + g * 256, 256), kf, f"wq{g % 2}")
                wload(wf[:, :, 256:384], wview(w_in, OXBC + 1024 + g * 128, 128), kf, f"wk{g % 2}")
                wload(wf[:, :, 384:512], wview(w_in, OXBC + 1536 + g * 128, 128), kf, f"wv{g % 2}")
                wload(wz[:], wview(w_in, OZ + g * 256, 256), kz, f"wo{g % 2}")
                return wf, kf, wz, kz

            carry = sb(esC, "carry", [128, 4, 4], BF16)
            ssq4 = Rot(esC, "ssq4", [128, 4], F32, 4)
            tbuf, ybuf = {}, {}
            tile_of = {c: k for k, (a_, b_) in enumerate(TILES) for c in range(a_, b_)}
            Srot = Rot(esC, "Srot", [128, 256], F32, 2)
            grp = {}

            def conv_job(g, k):
                def gen():
                    wf, kf = grp[g][0], grp[g][1]
                    c0, c1 = TILES[k]
                    n = (c1 - c0) * 128
                    blkch = [g * 2, g * 2 + 1, 8 + g, 12 + g]
                    for b in range(4):
                        cb_i = blkch[b]
                        pp, kp = pbank()
                        for kc in range(8):
                            op("pe", lambda e: e.matmul(pp[:, 0:n], lhsT=wf[:, kc, b * 128:(b + 1) * 128], rhs=uT[:, kc, c0 * 128:c1 * 128],
                                                        start=(kc == 0), stop=(kc == 7)), r=[kf] + uT_keys(c0, c1), w=[kp], inc=(kc == 7))
                        rw, krw = raw.next()
                        if k == 0:
                            op("pool", lambda e: e.memset(rw[:, 0:3], 0.0), w=[krw])
                        else:
                            op("pool", lambda e: e.tensor_copy(out=rw[:, 0:3], in_=carry[:, b, 0:3]), r=[("carry", b)], w=[krw])
                        op("act", lambda e: e.copy(out=rw[:, 3:3 + n], in_=pp[:, 0:n]), r=[kp], w=[krw])
                        if k < len(TILES) - 1:
                            op("pool", lambda e: e.tensor_copy(out=carry[:, b, 0:3], in_=rw[:, n:n + 3]), r=[krw], w=[("carry", b)])
                        yield
                        dgm, kdg = grp[g][6], grp[g][7]
                        pc, kpc = pbank()
                        for j in range(4):
                            op("pe", lambda e: e.matmul(pc[:, 0:n], lhsT=dgm[:, b * 4 + j, :], rhs=rw[:, j:j + n], start=(j == 0), stop=False),
                               r=[kdg, krw], w=[kpc], inc=False)
                        op("pe", lambda e: e.matmul(pc[:, 0:n], lhsT=cbrow[0:1, cb_i * 128:(cb_i + 1) * 128], rhs=onesb[0:1, 0:n],
                                                    start=False, stop=True), r=["cbrow", "onesb"], w=[kpc])
                        th_, kth = cth.next()
                        op("act", lambda e: e.activation(out=th_[:, 0:n], in_=pc[:, 0:n], func=AF.Tanh), r=[kpc], w=[kth])
                        op("dve", lambda e: e.scalar_tensor_tensor(out=fT[:, b, c0 * 128:c1 * 128], in0=th_[:, 0:n], scalar=1.0, in1=pc[:, 0:n],
                                                                  op0=ALU.add, op1=ALU.mult), r=[kth, kpc],
                           w=[("fT", b, c) for c in range(c0, c1)])
                        yield
                return gen

            def chunk_job(g, c):
                def gen():
                    wz, kz = grp[g][2], grp[g][3]
                    S, kS = grp[g][4], grp[g][5]
                    cs = slice(c * 128, (c + 1) * 128)
                    hs = slice(4 * g, 4 * g + 4)
                    b4 = lambda t: t[:, c, hs].unsqueeze(2).to_broadcast([128, 4, 64])
                    v3 = lambda ap: ap.rearrange("p (j d) -> p j d", j=4)
                    pt, kp = ptbank()
                    for j in range(2):
                        op("pe", lambda e: e.transpose(out=pt[:, j * 128:(j + 1) * 128], in_=fT[:, j, cs], identity=identb[:]),
                           r=[("fT", j, c), "identb"], w=[kp], inc=False)
                    op("pe", lambda e: e.transpose(out=pt[:, 256:384], in_=fT[:, 2, cs], identity=identb[:]),
                       r=[("fT", 2, c), "identb"], w=[kp])
                    xt_, kxt = xtk.next(); xw_, kxw = xw.next(); bk, kbk = btk.next()
                    op("act", lambda e: e.copy(out=xt_[:], in_=pt[:, 0:256]), r=[kp], w=[kxt])
                    op("act", lambda e: e.copy(out=bk[:], in_=pt[:, 256:384]), r=[kp], w=[kbk])
                    op("pool", lambda e: e.tensor_tensor(out=v3(xw_[:]), in0=v3(xt_[:]), in1=b4(dtw), op=ALU.mult), r=[kxt, "dtw"], w=[kxw])
                    yield
                    if c >= 1:
                        pz, kpz = pbank()
                        for kc in range(8):
                            op("pe", lambda e: e.matmul(pz[:, 0:256], lhsT=uT[:, kc, cs], rhs=wz[:, kc, :], start=(kc == 0), stop=(kc == 7)),
                               r=[kz, ("uT", c)], w=[kpz], inc=(kc == 7))
                        sz_, ksz = sz.next()
                        op("act", lambda e: e.activation(out=sz_[:], in_=pz[:, 0:256], func=AF.Tanh, scale=0.5), r=[kpz], w=[ksz])
                        op("dve", lambda e: e.scalar_tensor_tensor(out=sz_[:], in0=sz_[:], scalar=1.0, in1=pz[:, 0:256],
                                                                  op0=ALU.add, op1=ALU.mult), r=[ksz, kpz], w=[ksz])
                        pcb, kpcb = pbank()
                        op("pe", lambda e: e.matmul(pcb[:, 0:128], lhsT=fT[:, 2, cs], rhs=fT[:, 3, cs], start=True, stop=True),
                           r=[("fT", 2, c), ("fT", 3, c)], w=[kpcb])
                        cb_, kcb = cbs.next()
                        op("act", lambda e: e.copy(out=cb_[:], in_=pcb[:, 0:128]), r=[kpcb], w=[kcb])
                        pe_, kpe = pbank()
                        op("pe", lambda e: e.matmul(pe_[:], lhsT=ncfm[0:96, cs], rhs=sel[:, 4 * g * 128:(4 * g + 4) * 128], start=True, stop=False),
                           r=["ncfm", "sel"], w=[kpe], inc=False)
                        for j in range(4):
                            hd = 4 * g + j
                            op("pe", lambda e: e.matmul(pe_[:, j * 128:(j + 1) * 128], lhsT=sel[:, hd * 128:(hd + 1) * 128], rhs=cfm[0:96, cs],
                                                        start=False, stop=False), r=["cfm", "sel"], w=[kpe], inc=False)
                        op("pe", lambda e: e.matmul(pe_[:], lhsT=identb[:], rhs=negm[:], start=False, stop=True),
                           r=["identb", "negm"], w=[kpe])
                        dc, kdc = dec.next()
                        op("act", lambda e: e.activation(out=dc[:], in_=pe_[:], func=AF.Exp), r=[kpe], w=[kdc])
                    if c >= 1:
                        w4_, kw4 = w4.next()
                        op("dve", lambda e: e.tensor_tensor(out=w4_[:].rearrange("p (j t) -> p j t", j=4),
                                                            in0=dc[:].rearrange("p (j t) -> p j t", j=4),
                                                            in1=cb_[:].unsqueeze(1).to_broadcast([128, 4, 128]), op=ALU.mult),
                           r=[kdc, kcb], w=[kw4])
                        sb_, ksb = Sb.next()
                        op("act", lambda e: e.copy(out=sb_[:], in_=S[:]), r=[kS], w=[ksb])
                        yield
                        po, kpo = pbank()
                        op("pe", lambda e: e.matmul(po[:, 0:256], lhsT=fT[:, 3, cs], rhs=sb_[:], start=True, stop=True),
                           r=[("fT", 3, c), ksb], w=[kpo])
                        y_, ky = yt.next()
                        op("dve", lambda e: e.tensor_tensor(out=v3(y_[:]), in0=v3(po[:, 0:256]), in1=b4(ecum), op=ALU.mult), r=[kpo, "ecum"], w=[ky])
                        py, kpy = pbank()
                        for j in range(4):
                            op("pe", lambda e: e.matmul(py[:, j * 64:(j + 1) * 64], lhsT=w4_[:, j * 128:(j + 1) * 128], rhs=xt_[:, j * 64:(j + 1) * 64],
                                                        start=True, stop=False), r=[kw4, kxt], w=[kpy], inc=False)
                            op("pe", lambda e: e.matmul(py[:, j * 64:(j + 1) * 64], lhsT=DIm[:, 4 * g + j, :], rhs=xt_[:, j * 64:(j + 1) * 64],
                                                        start=False, stop=True), r=[("DIm", 4 * g + j), kxt], w=[kpy], inc=(j == 3))
                        op("dve", lambda e: e.tensor_tensor(out=y_[:], in0=y_[:], in1=py[:, 0:256], op=ALU.add), r=[ky, kpy], w=[ky])
                    if c < NCH - 1:
                        pst, kst = pbank()
                        op("pe", lambda e: e.matmul(pst[:, 0:256], lhsT=bk[:], rhs=xw_[:], start=True, stop=True), r=[kbk, kxw], w=[kst])
                        if c == 0:
                            op("dve", lambda e: e.tensor_copy(out=S[:], in_=pst[:, 0:256]), r=[kst], w=[kS])
                        else:
                            op("pool", lambda e: e.tensor_tensor(out=v3(S[:]), in0=v3(S[:]), in1=b4(chdec), op=ALU.mult), r=[kS, "chdec"], w=[kS])
                            op("dve", lambda e: e.tensor_tensor(out=S[:], in0=S[:], in1=pst[:, 0:256], op=ALU.add), r=[kS, kst], w=[kS])
                    if c == 0:
                        return
                    yield
                    op("dve", lambda e: e.tensor_tensor(out=y_[:], in0=y_[:], in1=sz_[:], op=ALU.mult), r=[ky, ksz], w=[ky])
                    k_ = tile_of[c]
                    if (g, k_) not in tbuf:
                        tbuf[(g, k_)] = ssq4.next()
                    s4, ks4 = tbuf[(g, k_)]
                    cc = c - max(TILES[k_][0], 1)
                    junk, kj = rotsC["junk"].next()
                    op("act", lambda e: e.activation(out=junk[:], in_=y_[:], func=AF.Square, accum_out=s4[:, cc:cc + 1]),
                       r=[ky], w=[kj, (ks4, cc)])
                    ybuf[(g, c)] = (y_, ky)
                return gen

            def back_job(g, k):
                def gen():
                    c0 = max(TILES[k][0], 1); c1 = TILES[k][1]; ncc = c1 - c0
                    s4, ks4 = tbuf[(g, k)]
                    k4 = [(ks4, cc) for cc in range(ncc)]
                    op("dve", lambda e: e.tensor_scalar(out=s4[:, 0:ncc], in0=s4[:, 0:ncc], scalar1=1.0 / 256, scalar2=4 * EPS,
                                                        op0=ALU.mult, op1=ALU.add), r=k4, w=k4)
                    op("act", lambda e: e.activation(out=s4[:, 0:ncc], in_=s4[:, 0:ncc], func=AF.Sqrt), r=k4, w=k4)
                    yield
                    op("dve", lambda e: e.reciprocal(out=s4[:, 0:ncc], in_=s4[:, 0:ncc]), r=k4, w=k4)
                    yield
                    ybs = []
                    for cc in range(ncc):
                        y_, ky = ybuf.pop((g, c0 + cc))
                        yb, kyb = ysb.next()
                        op("dve", lambda e: e.scalar_tensor_tensor(out=yb[:], in0=y_[:], scalar=s4[:, cc:cc + 1], in1=sn[:, g * 256:(g + 1) * 256],
                                                                  op0=ALU.mult, op1=ALU.mult), r=[ky, (ks4, cc), "snbc"], w=[kyb])
                        yt.release(ky)
                        ybs.append((yb, kyb))
                    yield
                    pt2, kp2 = ptbank()
                    last = (ncc - 1, 1)
                    for cc in range(ncc):
                        yb, kyb = ybs[cc]
                        for j in range(2):
                            col = (j * ncc + cc) * 128
                            op("pe", lambda e: e.transpose(out=pt2[:, col:col + 128], in_=yb[:, j * 128:(j + 1) * 128],
                                                           identity=identb[:]), r=[kyb, "identb"], w=[kp2], inc=((cc, j) == last))
                    op("act", lambda e: e.copy(out=ysT[:, 2 * g:2 * g + 2, (c0 - 1) * 128:(c1 - 1) * 128],
                                               in_=pt2[:, 0:2 * ncc * 128].rearrange("p (k t) -> p k t", k=2)), r=[kp2],
                       w=[("ysT", g, c) for c in range(c0, c1)])
                return gen

            def loadg_job(g):
                def gen():
                    S_, kS_ = Srot.next()
                    dgm, kdg = dgms.next()
                    blkch = [g * 2, g * 2 + 1, 8 + g, 12 + g]
                    for b in range(4):
                        for j in range(4):
                            op("dve", lambda e: e.tensor_scalar(out=dgm[:, b * 4 + j, :], in0=identf[:], scalar1=cw[:, blkch[b], j:j + 1],
                                                                scalar2=None, op0=ALU.mult), r=["identf", "cw"], w=[kdg])
                    grp[g] = load_group(g) + (S_, kS_, dgm, kdg)
                    return
                    yield
                return gen

            jobs = []
            J = lambda fn, deps=(), cls='chunk': (jobs.append((fn, deps, cls)), len(jobs) - 1)[1]
            J(loadg_job(0), "drain"); J(loadg_job(1), "drain")
            cv = {}
            fj = []
            cv[(0, 0)] = J(conv_job(0, 0), (), 'conv'); cv[(0, 1)] = J(conv_job(0, 1), (), 'conv')
            J(gates_gen, "bg")
            for g in range(4):
                for k in range(5):
                    c0, c1 = TILES[k]
                    for c in range(c0, c1):
                        if k == 4 and g < 3:
                            cv[(g + 1, 0)] = J(conv_job(g + 1, 0), (), 'conv'); cv[(g + 1, 1)] = J(conv_job(g + 1, 1), (), 'conv')
                        fj.append(J(chunk_job(g, c), (cv[(g, k)],)))
                    if k + 2 < 5:
                        cv[(g, k + 2)] = J(conv_job(g, k + 2), (), 'conv')
                    if k >= 1:
                        kk = k - 1
                        n0 = max(TILES[kk][0], 1)
                        J(back_job(g, kk), tuple(fj[g * NCH + c_] for c_ in range(n0, TILES[kk][1])), 'back')
                J(back_job(g, 4), (fj[g * NCH + 16],), 'back')
                if g + 2 < 4:
                    J(loadg_job(g + 2), "drain")
            run_jobs(jobs, 7, {'chunk': 4, 'conv': 2, 'back': 1})
            tk.barrier()
        maybe_stop("C")
        if debug:
            dma("pool", "dbgp", dbg["d_ysT"], ysT[:].rearrange("p k t -> p (k t)"),
                r=[("ysT", g, c) for g in range(4) for c in range(1, NCH)])

        sD0 = ExitStack(); sD0.__enter__()
        wd = Rot(sD0, "wd", [128, 8, 512], BF16, 1)

        def load_d(d):
            w_, kw = wd.next()
            wload(w_[:, :, 0:128], wview(m_proj, d * 128, 128), kw, f"dq{d % 2}")
            wload(w_[:, :, 128:256], wview(s_proj, d * 128, 128), kw, f"dk{d % 2}")
            wload(w_[:, :, 256:384], wview(w_in, OGA + d * 128, 128), kw, f"dv{d % 2}")
            wload(w_[:, :, 384:512], wview(w_in, OGB + d * 128, 128), kw, f"do{d % 2}")
            return w_, kw
        d_pre = [load_d(0)]

        esB = ExitStack()
        with esB:
            gn = load_bc(esB, "gnbc", m_norm_g, 1024, "c0")
            op("dve", lambda e: e.tensor_scalar(out=gn[:], in0=gn[:], scalar1=0.5, scalar2=None, op0=ALU.mult), r=["gnbc"], w=["gnbc"])
            wfm = Rot(esB, "wfm", [128, 8, 256], BF16, 2)
            wtk = ARot("wtk", R2, [128, 8, 512], BF16, 2)
            qT = aview(wtk.end, [128, T]); kT = aview(wtk.end + 2 * T, [128, T])
            Csb = Rot(esB, "Csb", [128, 257], BF16, 6)
            vaug = Rot(esB, "vaug", [128, 257], BF16, 6); ktok = Rot(esB, "ktok", [128, 128], BF16, 6)
            so = Rot(esB, "so", [128, 256], F32, 14, hold=True); sm = Rot(esB, "sm", [128, 128], BF16, 6)
            hmb = Rot(esB, "hmb", [128, 256], BF16, 5)
            rotsB = {"junk": Rot(esB, "junkB", [128, 256], BF16, 2)}
            for v_t in vaug.t:
                op("dve", lambda e: e.memset(v_t[:, 256:257], 1.0), w=[("vaug", vaug.t.index(v_t))])

            def load_head(h):
                wf, kf = wfm.next(); wt, kt = wtk.next()
                wload(wf[:, :, 0:128], wview(w_in, OQ + h * 128, 128), kf, f"wq{h % 2}")
                wload(wf[:, :, 128:256], wview(w_in, OK_ + h * 128, 128), kf, f"wk{h % 2}")
                wload(wt[:, :, 0:256], wview(w_in, OV + h * 256, 256), kt, f"wv{h % 2}")
                wload(wt[:, :, 256:512], wview(w_in, OO + h * 256, 256), kt, f"wo{h % 2}")
                return (wf, kf, wt, kt)

            CTrot = Rot(esB, "CTrot", [128, 257], F32, 2)
            numS = Rot(esB, "numS", [128, 257], F32, 14, hold=True)
            ssq4B = Rot(esB, "ssq4B", [128, 4], F32, 5); d14 = Rot(esB, "d14", [128, 4], F32, 5)
            tbufB, nbuf = {}, {}
            tile_of = {c: k for k, (a_, b_) in enumerate(TILES) for c in range(a_, b_)}
            hd_ = {}

            def loadh_job(h):
                def gen():
                    C_, kC_ = CTrot.next()
                    hd_[h] = load_head(h) + (C_, kC_)
                    return
                    yield
                return gen

            def qk_job(h, k):
                def gen():
                    wf, kf = hd_[h][0], hd_[h][1]
                    c0, c1 = TILES[k]
                    n = (c1 - c0) * 128
                    for which, dst, nm in ((0, qT, "qT"), (1, kT, "kT")):
                        pp, kp = pbank()
                        for kc in range(8):
                            op("pe", lambda e: e.matmul(pp[:, 0:n], lhsT=wf[:, kc, which * 128:(which + 1) * 128],
                                                        rhs=uT[:, kc, c0 * 128:c1 * 128], start=(kc == 0), stop=(kc == 7)),
                               r=[kf] + uT_keys(c0, c1), w=[kp], inc=(kc == 7))
                        if which == 0:
                            op("act", lambda e: e.activation(out=dst[:, c0 * 128:c1 * 128], in_=pp[:, 0:n], func=AF.Copy,
                                                             scale=float(128 ** -0.5)), r=[kp], w=[(nm, c) for c in range(c0, c1)])
                        else:
                            op("dve", lambda e: e.tensor_copy(out=dst[:, c0 * 128:c1 * 128], in_=pp[:, 0:n]), r=[kp],
                               w=[(nm, c) for c in range(c0, c1)])
                        yield
                return gen

            def mchunk_job(h, c):
                def gen():
                    wt, kt = hd_[h][2], hd_[h][3]
                    CT, kCT = hd_[h][4], hd_[h][5]
                    cs = slice(c * 128, (c + 1) * 128)
                    eacol = ea[:, c, h:h + 1]
                    pvo, kvo = pbank()
                    for kc in range(8):
                        op("pe", lambda e: e.matmul(pvo[:], lhsT=uT[:, kc, cs], rhs=wt[:, kc, 0:512], start=(kc == 0), stop=(kc == 7)),
                           r=[kt, ("uT", c)], w=[kvo], inc=(kc == 7))
                    pk, kpk = ptbank()
                    op("pe", lambda e: e.transpose(out=pk[:, 0:128], in_=kT[:, cs], identity=identb[:]), r=[("kT", c), "identb"], w=[kpk])
                    va, kva = vaug.next(); ktk, kkt = ktok.next()
                    op("act", lambda e: e.copy(out=va[:, 0:256], in_=pvo[:, 0:256]), r=[kvo], w=[kva])
                    op("act", lambda e: e.activation(out=ktk[:], in_=pk[:, 0:128], func=AF.Copy, scale=eacol), r=[kpk, "ea"], w=[kkt])
                    if c >= 1:
                        sot, kso = so.next()
                        op("act", lambda e: e.activation(out=sot[:], in_=pvo[:, 256:512], func=AF.Tanh, scale=0.5), r=[kvo], w=[kso])
                        op("dve", lambda e: e.scalar_tensor_tensor(out=sot[:], in0=sot[:], scalar=1.0, in1=gn[:, h * 256:(h + 1) * 256],
                                                                   op0=ALU.add, op1=ALU.mult), r=[kso, "gnbc"], w=[kso])
                    yield
                    if c >= 1:
                        psc, ksc = pbank()
                        op("pe", lambda e: e.matmul(psc[:, 0:128], lhsT=kT[:, cs], rhs=qT[:, cs], start=True, stop=True),
                           r=[("kT", c), ("qT", c)], w=[ksc])
                        smt, ksm = sm.next()
                        op("dve", lambda e: e.scalar_tensor_tensor(out=smt[:], in0=psc[:, 0:128], scalar=eacol, in1=tri[:],
                                                                  op0=ALU.mult, op1=ALU.mult), r=[ksc, "ea", "tri"], w=[ksm])
                        cb_, kcb = Csb.next()
                        op("act", lambda e: e.activation(out=cb_[:], in_=CT[:], func=AF.Copy, scale=sc[:, c, h:h + 1]),
                           r=[kCT, "sc"], w=[kcb])
                        yield
                        pn, kpn = pbank()
                        op("pe", lambda e: e.matmul(pn[:, 0:257], lhsT=smt[:], rhs=va[:], start=True, stop=False),
                           r=[ksm, kva], w=[kpn], inc=False)
                        op("pe", lambda e: e.matmul(pn[:, 0:257], lhsT=qT[:, cs], rhs=cb_[:], start=False, stop=True),
                           r=[("qT", c), kcb], w=[kpn])
                        nS, knS = numS.next()
                        op("act", lambda e: e.copy(out=nS[:], in_=pn[:, 0:257]), r=[kpn], w=[knS])
                    if c < NCH - 1:
                        pst, kst = pbank()
                        op("pe", lambda e: e.matmul(pst[:, 0:257], lhsT=ktk[:], rhs=va[:], start=True, stop=True),
                           r=[kkt, kva], w=[kst])
                        if c == 0:
                            op("dve", lambda e: e.tensor_copy(out=CT[:], in_=pst[:, 0:257]), r=[kst], w=[kCT])
                        else:
                            op("dve", lambda e: e.scalar_tensor_tensor(out=CT[:], in0=CT[:], scalar=sc[:, c, h:h + 1], in1=pst[:, 0:257],
                                                                      op0=ALU.mult, op1=ALU.add), r=[kst, kCT, "sc"], w=[kCT])
                    if c == 0:
                        return
                    yield
                    k_ = tile_of[c]
                    if (h, k_) not in tbufB:
                        tbufB[(h, k_)] = (ssq4B.next(), d14.next())
                    (s4, ks4), (d4, kd4) = tbufB[(h, k_)]
                    cc = c - max(TILES[k_][0], 1)
                    d1 = d4[:, cc:cc + 1]; kd1 = (kd4, cc)
                    op("act", lambda e: e.activation(out=d1, in_=nS[:, 256:257], func=AF.Abs), r=[knS], w=[kd1])
                    op("dve", lambda e: e.tensor_scalar(out=d1, in0=d1, scalar1=dn[:, c, h:h + 1], scalar2=None,
                                                        op0=ALU.max), r=[kd1, "dn"], w=[kd1])
                    op("dve", lambda e: e.reciprocal(out=d1, in_=d1), r=[kd1], w=[kd1])
                    yield
                    junk, kj = rotsB["junk"].next()
                    op("act", lambda e: e.activation(out=junk[:], in_=nS[:, 0:256], func=AF.Square, scale=d1, accum_out=s4[:, cc:cc + 1]),
                       r=[knS, kd1], w=[kj, (ks4, cc)])
                    nbuf[(h, c)] = (nS, knS, sot, kso)
                return gen

            def mback_job(h, k):
                def gen():
                    c0 = max(TILES[k][0], 1); c1 = TILES[k][1]; ncc = c1 - c0
                    (s4, ks4), (d4, kd4) = tbufB[(h, k)]
                    k4 = [(ks4, cc) for cc in range(ncc)]; kd = [(kd4, cc) for cc in range(ncc)]
                    op("dve", lambda e: e.tensor_scalar(out=s4[:, 0:ncc], in0=s4[:, 0:ncc], scalar1=1.0 / 256, scalar2=EPS,
                                                        op0=ALU.mult, op1=ALU.add), r=k4, w=k4)
                    op("act", lambda e: e.activation(out=s4[:, 0:ncc], in_=s4[:, 0:ncc], func=AF.Sqrt), r=k4, w=k4)
                    yield
                    op("dve", lambda e: e.reciprocal(out=s4[:, 0:ncc], in_=s4[:, 0:ncc]), r=k4, w=k4)
                    op("dve", lambda e: e.tensor_tensor(out=s4[:, 0:ncc], in0=s4[:, 0:ncc], in1=d4[:, 0:ncc], op=ALU.mult), r=k4 + kd, w=k4)
                    yield
                    hbs = []
                    for cc in range(ncc):
                        nS, knS, sot, kso = nbuf.pop((h, c0 + cc))
                        hb, khb = hmb.next()
                        op("dve", lambda e: e.scalar_tensor_tensor(out=hb[:], in0=nS[:, 0:256], scalar=s4[:, cc:cc + 1], in1=sot[:],
                                                                  op0=ALU.mult, op1=ALU.mult), r=[knS, (ks4, cc), kso], w=[khb])
                        numS.release(knS); so.release(kso)
                        hbs.append((hb, khb))
                    yield
                    pt, kp = ptbank()
                    last = (ncc - 1, 1)
                    for cc in range(ncc):
                        hb, khb = hbs[cc]
                        for j in range(2):
                            col = (j * ncc + cc) * 128
                            op("pe", lambda e: e.transpose(out=pt[:, col:col + 128], in_=hb[:, j * 128:(j + 1) * 128],
                                                           identity=identb[:]), r=[khb, "identb"], w=[kp], inc=((cc, j) == last))
                    op("act", lambda e: e.copy(out=hmT[:, 2 * h:2 * h + 2, (c0 - 1) * 128:(c1 - 1) * 128],
                                               in_=pt[:, 0:2 * ncc * 128].rearrange("p (k t) -> p k t", k=2)), r=[kp],
                       w=[("hmT", h, c) for c in range(c0, c1)])
                return gen

            jobs = []
            J = lambda fn, deps=(), cls='chunk': (jobs.append((fn, deps, cls)), len(jobs) - 1)[1]
            J(loadh_job(0), "drain"); J(loadh_job(1), "drain")
            qj = {}
            fjB = []
            qj[(0, 0)] = J(qk_job(0, 0), (), 'qk'); qj[(0, 1)] = J(qk_job(0, 1), (), 'qk')
            for h in range(4):
                for k in range(5):
                    c0, c1 = TILES[k]
                    for c in range(c0, c1):
                        if k == 4 and h < 3:
                            qj[(h + 1, 0)] = J(qk_job(h + 1, 0), (), 'qk'); qj[(h + 1, 1)] = J(qk_job(h + 1, 1), (), 'qk')
                        fjB.append(J(mchunk_job(h, c), (qj[(h, k)],)))
                    if k + 2 < 5:
                        qj[(h, k + 2)] = J(qk_job(h, k + 2), (), 'qk')
                    if k >= 1:
                        kk = k - 1
                        n0 = max(TILES[kk][0], 1)
                        J(mback_job(h, kk), tuple(fjB[h * NCH + c_] for c_ in range(n0, TILES[kk][1])), 'back')
                J(mback_job(h, 4), (fjB[h * NCH + 16],), 'back')
                if h + 2 < 4:
                    J(loadh_job(h + 2), "drain")
            run_jobs(jobs, 9, {'chunk': 6, 'qk': 2, 'back': 1})
            tk.barrier()
        maybe_stop("B")
        if debug:
            dma("pool", "dbgp", dbg["d_hmT"], hmT[:].rearrange("p k t -> p (k t)"),
                r=[("hmT", h, c) for h in range(4) for c in range(1, NCH)])

        esD = ExitStack()
        with esD:
            sg = Rot(esD, "sg", [128, 512], F32, 4)
            wd.t.append(sb(esD, "wd1b", [128, 8, 512], BF16))
            d_pre.append(load_d(1))
            for d in range(8):
                w_, kw = d_pre[d] if d < 2 else nxt
                if 2 <= d + 1 < 8:
                    nxt = load_d(d + 1)
                for tl in range(4):
                    ts_ = slice(tl * 512, (tl + 1) * 512)
                    tu = slice(128 + tl * 512, 128 + (tl + 1) * 512)
                    hk = [("hmT", h, c) for h in range(4) for c in range(1 + tl * 4, 5 + tl * 4)]
                    yk = [("ysT", g, c) for g in range(4) for c in range(1 + tl * 4, 5 + tl * 4)]
                    uk = uT_keys(1 + tl * 4, 5 + tl * 4)
                    ps_ = []
                    for i, (src, keys) in enumerate(((hmT, hk), (ysT, yk), (uT, uk), (uT, uk))):
                        pp, kp = pbank()
                        sl = ts_ if i < 2 else tu
                        for kc in range(8):
                            op("pe", lambda e: e.matmul(pp[:], lhsT=w_[:, kc, i * 128:(i + 1) * 128], rhs=src[:, kc, sl],
                                                        start=(kc == 0), stop=(kc == 7)), r=[kw] + keys, w=[kp], inc=(kc == 7))
                        ps_.append((pp, kp))
                    sa, ksa = sg.next(); sb2, ksb2 = sg.next()
                    op("act", lambda e: e.activation(out=sa[:], in_=ps_[2][0][:], func=AF.Sigmoid), r=[ps_[2][1]], w=[ksa])
                    op("act", lambda e: e.activation(out=sb2[:], in_=ps_[3][0][:], func=AF.Sigmoid), r=[ps_[3][1]], w=[ksb2])
                    op("dve", lambda e: e.tensor_tensor(out=sa[:], in0=sa[:], in1=ps_[0][0][:], op=ALU.mult), r=[ksa, ps_[0][1]], w=[ksa])
                    op("dve", lambda e: e.tensor_tensor(out=sb2[:], in0=sb2[:], in1=ps_[1][0][:], op=ALU.mult), r=[ksb2, ps_[1][1]], w=[ksb2])
                    op("pool", lambda e: e.tensor_tensor(out=mgT[:, d, ts_], in0=sa[:], in1=sb2[:], op=ALU.add), r=[ksa, ksb2],
                       w=[("mgT", d, c_) for c_ in range(tl * 4, tl * 4 + 4)])
            tk.barrier()
        maybe_stop("D")
        if debug:
            dma("pool", "dbgp", dbg["d_mgT"], mgT[:].rearrange("p k t -> p (k t)"), r=[("mgT", d, c_) for d in range(8) for c_ in range(16)])
        tk.barrier()
        sD0.__exit__(None, None, None)
        su.__exit__(None, None, None)

        sF0 = ExitStack(); sF0.__enter__()
        PASS = [(0, 4), (4, 8), (8, 12), (12, 16), (16, 19), (19, 22)]
        NP = len(PASS)
        wfi = Rot(sF0, "wfi", [128, 8, 2 * 512], BF16, 2)
        wfo = Rot(sF0, "wfo", [128, 4, 1024], BF16, 2)

        def load_pass(p):
            f0, f1 = PASS[p]; nf = f1 - f0
            wi, ki = wfi.next(); wo_, ko = wfo.next()
            wload(wi[:, :, 0:nf * 128], wview(w_ffn_in, f0 * 128, nf * 128), ki, f"fq{p % 2}")
            wload(wi[:, :, 512:512 + nf * 128], wview(w_ffn_in, FF + f0 * 128, nf * 128), ki, f"fk{p % 2}")
            wload(wo_[:, 0:nf, :], w_ffn_out[f0 * 128:f1 * 128, :].rearrange("(f p) n -> p f n", p=128), ko, f"fv{p % 2}")
            return wi, ki, wo_, ko

        esE = ExitStack()
        with esE:
            wo = sb(esE, "wo_", [128, 8, 1024], BF16)
            for i in range(2):
                wload(wo[:, :, i * 512:(i + 1) * 512], wview(w_out, i * 512, 512), ("wo_", i), f"wq{i}")
            p_pre = [load_pass(0), load_pass(1)]
            g2 = load_bc(esE, "g2bc", norm2_g, 1024, "c0")
            rots = {"junk": Rot(esE, "junkF", [128, 1024], BF16, 2), "ssq": Rot(esE, "ssqF", [128, 1], F32, 8),
                    "rstd": Rot(esE, "rstdF", [128, 1], F32, 8), "ub": Rot(esE, "ubF", [128, 1024], BF16, 4)}
            xin = Rot(esE, "xinE", [128, 1024], F32, 4)

            def e_job(c):
                def gen():
                    xt, kx = xin.next()
                    dma("sp", f"x{c % 4}", xt[:], x[c * 128:(c + 1) * 128, :], w=[kx])
                    mk = [("mgT", d, c) for d in range(8)]
                    for hf in range(2):
                        pp, kp = pbank()
                        for kc in range(8):
                            op("pe", lambda e: e.matmul(pp[:], lhsT=mgT[:, kc, c * 128:(c + 1) * 128], rhs=wo[:, kc, hf * 512:(hf + 1) * 512],
                                                        start=(kc == 0), stop=(kc == 7)), r=[("wo_", hf)] + mk, w=[kp], inc=(kc == 7))
                        op("dve", lambda e: e.tensor_tensor(out=h2[:, c, hf * 512:(hf + 1) * 512], in0=pp[:], in1=xt[:, hf * 512:(hf + 1) * 512],
                                                            op=ALU.add), r=[kp, kx], w=[("h2", c, hf)])
                    yield
                    yield from norm_to_T(rots, h2[:, c, :], [("h2", c, 0), ("h2", c, 1)], g2, "g2bc", u2T, c, [("u2T", c)] + mk)
                return gen
            run_jobs([(e_job(c), ()) for c in range(16)], 4)
            tk.barrier()
        maybe_stop("E")
        if debug:
            dma("sp", "dbg", dbg["d_h2"], h2[:].rearrange("p c d -> p (c d)"), r=[("h2", c, hf) for c in range(16) for hf in range(2)])

        esF = ExitStack()
        with esF:
            gf = load_bc(esF, "gfbc", norm_f_g, 1024, "c0")
            rotsG = {"junk": Rot(esF, "junkG", [128, 1024], BF16, 2), "ssq": Rot(esF, "ssqG", [128, 1], F32, 8),
                     "rstd": Rot(esF, "rstdG", [128, 1], F32, 8)}
            ob = Rot(esF, "ob", [128, 1024], F32, 2)

            def g_job(c):
                def gen():
                    rs, kr = rms_rstd(rotsG, h2[:, c, :], [("h2", c, 0), ("h2", c, 1)], 1024)
                    yield
                    o_, ko_ = ob.next()
                    op("dve", lambda e: e.scalar_tensor_tensor(out=o_[:], in0=h2[:, c, :], scalar=rs[:, 0:1], in1=gf[:], op0=ALU.mult, op1=ALU.mult),
                       r=[("h2", c, 0), ("h2", c, 1), kr, "gfbc"], w=[ko_])
                    yield
                    dma("sp", f"o{c % 2}", out[c * 128:(c + 1) * 128, :], o_[:], r=[ko_])
                return gen
            pending = []

            def advance():
                for g_ in list(pending):
                    try:
                        next(g_)
                    except StopIteration:
                        pending.remove(g_)
            actT = Rot(esF, "actT", [128, 4, 512], BF16, 2)
            sgF = Rot(esF, "sgF", [128, 512], F32, 3)
            for p in range(NP):
                f0, f1 = PASS[p]; nf = f1 - f0
                wi, ki, wo_, ko = p_pre[p] if p < 2 else nxt
                if 2 <= p + 1 < NP:
                    nxt = load_pass(p + 1)
                for tl in range(4):
                    tu = slice(tl * 512, (tl + 1) * 512)
                    uk = [("u2T", c_) for c_ in range(tl * 4, tl * 4 + 4)]
                    at, kat = actT.next()
                    for i in range(nf):
                        pg_, kpg_ = pbank()
                        for kc in range(8):
                            op("pe", lambda e: e.matmul(pg_[:], lhsT=wi[:, kc, i * 128:(i + 1) * 128], rhs=u2T[:, kc, tu],
                                                        start=(kc == 0), stop=(kc == 7)), r=[ki] + uk, w=[kpg_], inc=(kc == 7))
                        pu_, kpu_ = pbank()
                        for kc in range(8):
                            op("pe", lambda e: e.matmul(pu_[:], lhsT=wi[:, kc, 512 + i * 128:512 + (i + 1) * 128], rhs=u2T[:, kc, tu],
                                                        start=(kc == 0), stop=(kc == 7)), r=[ki] + uk, w=[kpu_], inc=(kc == 7))
                        s_, ks_ = sgF.next()
                        op("act", lambda e: e.activation(out=s_[:], in_=pg_[:], func=AF.Silu), r=[kpg_], w=[ks_])
                        op("dve", lambda e: e.tensor_tensor(out=at[:, i, :], in0=s_[:], in1=pu_[:], op=ALU.mult), r=[ks_, kpu_], w=[kat])
                    for cc in range(4):
                        c = tl * 4 + cc
                        for hf in range(2):
                            pp, kp = pbank()
                            for i in range(nf):
                                op("pe", lambda e: e.matmul(pp[:], lhsT=at[:, i, cc * 128:(cc + 1) * 128], rhs=wo_[:, i, hf * 512:(hf + 1) * 512],
                                                            start=(i == 0), stop=(i == nf - 1)), r=[kat, ko], w=[kp], inc=(i == nf - 1))
                            op("dve", lambda e: e.tensor_tensor(out=h2[:, c, hf * 512:(hf + 1) * 512], in0=h2[:, c, hf * 512:(hf + 1) * 512],
                                                                in1=pp[:], op=ALU.add), r=[kp, ("h2", c, hf)], w=[("h2", c, hf)])
                        if p == NP - 1:
                            pending.append(g_job(c)())
                        advance()
            while pending:
                advance()
            tk.barrier()
        sF0.__exit__(None, None, None)

    except _Stop:
        pass
    return nc


def host_inputs(x, meta, norm1_g, w_in, m_igate_b, m_fgate_b, m_norm_g, m_proj, s_conv_w, s_conv_b,
                s_dt_bias, s_A_log, s_D, s_norm_g, s_proj, w_out, norm2_g, w_ffn_in, w_ffn_out, norm_f_g):
    f = lambda a: np.ascontiguousarray(np.asarray(a, dtype=np.float32))
    h0 = np.zeros((128, 1024), np.float32); h0[112:] = f(meta)
    tri = np.triu(np.ones((128, 128), np.float32))
    negm = np.tile(np.where(tri > 0, 0.0, NEG).astype(np.float32), (1, 4))
    sel = np.zeros((96, 16, 128), np.float32)
    for k in range(16):
        for r_ in range(3):
            sel[32 * r_ + k, k, :] = 1.0
    valid = (np.arange(128) >= 112).astype(np.float32).reshape(128, 1)
    shared = {
        "h0": h0, "w_in": f(w_in[0]), "m_proj": f(m_proj[0]), "s_proj": f(s_proj[0]), "w_out": f(w_out[0]),
        "w_ffn_in": f(w_ffn_in[0]), "w_ffn_out": f(w_ffn_out[0]),
        "norm1_g": f(norm1_g[0]), "norm2_g": f(norm2_g[0]), "norm_f_g": f(norm_f_g),
        "m_norm_g": f(m_norm_g[0]).reshape(1024), "s_norm_g": f(s_norm_g[0]),
        "gate_b": np.concatenate([f(m_igate_b[0]), f(m_fgate_b[0])]), "dt_bias": f(s_dt_bias[0]), "a_log": f(s_A_log[0]),
        "s_d": f(s_D[0]), "convw": f(f(s_conv_w[0]).T), "convb": f(f(s_conv_b[0]).reshape(1, 2048)),
        "c_ident": np.eye(128, dtype=np.float32), "c_tri": tri, "c_negm": negm, "c_sel": sel.reshape(96, 2048), "c_valid": valid,
    }
    xs = f(x)
    return [dict(shared, x=xs[b]) for b in range(8)]


def kernel(**inputs):
    in_maps = host_inputs(**inputs)
    if "nc" not in _NC:
        _NC["nc"] = build(False)
    res = run_bass_kernel_spmd(_NC["nc"], in_maps, core_ids=list(range(8)))
    return np.stack([np.asarray(r["out"], dtype=np.float32).reshape(2048, 1024) for r in res.results], axis=0)
```

```python
import numpy as np
from contextlib import ExitStack
import concourse.bass as bass
import concourse.mybir as mybir
from concourse.bass_utils import run_bass_kernel_spmd

F32 = mybir.dt.float32
BF16 = mybir.dt.bfloat16
AF = mybir.ActivationFunctionType
ALU = mybir.AluOpType
AX = mybir.AxisListType

NCH = 17
T = NCH * 128
EPS = 1e-6
NEG = -30000.0
OQ, OK_, OV, OO, OI, OF, OZ, OXBC, ODT, OGA, OGB = 0, 512, 1024, 2048, 3072, 3076, 3080, 4104, 6152, 6168, 7192
FF = 2816


_NC = {}


class Dom:
    def __init__(self, name, sem):
        self.name, self.sem, self.count, self.snaps = name, sem, 0, {}


class Trk:
    def __init__(self, nc, es):
        self.nc, self.es = nc, es
        self.eng = {"pe": nc.tensor, "dve": nc.vector, "act": nc.scalar, "pool": nc.gpsimd, "sp": nc.sync}
        self.dom = {k: Dom(k, es.enter_context(nc.semaphore("s_" + k))) for k in self.eng}
        self.known = {k: {} for k in self.eng}
        self.lastw, self.readers = {}, {}
        self.slots = {}
        self.nslot = 0
        self.log = {k: [] for k in self.eng}

    def _need(self, e, tok):
        d, v = tok
        if self.known[e].get(d.name, 0) >= v:
            return
        if e == "pe" and d.name == "pe":
            return
        self.eng[e].wait_ge(d.sem, v)
        self.log[e].append(("w", d.name, v))
        self.known[e][d.name] = v
        snap = d.snaps.get(v)
        if snap:
            for k2, v2 in snap.items():
                if self.known[e].get(k2, 0) < v2:
                    self.known[e][k2] = v2

    def _deps(self, e, r, w):
        toks = {}
        def add(t):
            if t is None:
                return
            d, v = t
            if d.name not in toks or toks[d.name][1] < v:
                toks[d.name] = t
        for k in r:
            add(self.lastw.get(k))
        for k in w:
            add(self.lastw.get(k))
            for t in self.readers.get(k, ()):
                add(t)
        for t in toks.values():
            self._need(e, t)

    def _commit(self, tok, r, w):
        for k in r:
            self.readers.setdefault(k, []).append(tok)
        for k in w:
            self.lastw[k] = tok
            self.readers[k] = []

    def op(self, e, emit, r=(), w=(), inc=True):
        self._deps(e, r, w)
        inst = emit(self.eng[e])
        d = self.dom[e]
        if inc:
            d.count += 1
            inst.then_inc(d.sem, 1)
            self.log[e].append(("i", d.name, 1))
            tok = (d, d.count)
            self.known[e][d.name] = max(self.known[e].get(d.name, 0), 0)
            d.snaps[d.count] = dict(self.known[e])
        else:
            tok = (d, d.count + 1)
        self._commit(tok, r, w)
        return tok

    def dma(self, e, slot, out, in_, r=(), w=()):
        if slot not in self.slots:
            self.slots[slot] = Dom("dma_" + slot, self.es.enter_context(self.nc.semaphore("d_" + slot)))
        d = self.slots[slot]
        if d.count:
            self._need(e, (d, d.count))
        self._deps(e, r, w)
        inst = self.eng[e].dma_start(out=out, in_=in_)
        d.count += 16
        inst.then_inc(d.sem, 16)
        self.log[e].append(("i", d.name, 16))
        tok = (d, d.count)
        self._commit(tok, r, w)
        return tok

    def check(self):
        sem = {}
        pc = {e: 0 for e in self.log}
        prog = True
        while prog:
            prog = False
            for e, lg in self.log.items():
                while pc[e] < len(lg):
                    k, dn, v = lg[pc[e]]
                    if k == "w":
                        if sem.get(dn, 0) < v:
                            break
                    else:
                        sem[dn] = sem.get(dn, 0) + v
                    pc[e] += 1
                    prog = True
        stuck = {e: (pc[e], len(lg), lg[pc[e]], sem.get(lg[pc[e]][1], 0)) for e, lg in self.log.items() if pc[e] < len(lg)}
        return stuck

    def barrier(self):
        for e in self.eng:
            for d in list(self.dom.values()) + list(self.slots.values()):
                if d.count:
                    self._need(e, (d, d.count))


class _Stop(Exception):
    pass


def build(debug=False, stop=None):
    nc = bass.Bass("TRN2", target_bir_lowering=False)
    dr = lambda n, s, k="ExternalInput": nc.dram_tensor(n, s, F32, kind=k).ap()
    x = dr("x", [2048, 1024]); h0 = dr("h0", [128, 1024])
    w_in = dr("w_in", [1024, 8216]); m_proj = dr("m_proj", [1024, 1024]); s_proj = dr("s_proj", [1024, 1024])
    w_out = dr("w_out", [1024, 1024]); w_ffn_in = dr("w_ffn_in", [1024, 2 * FF]); w_ffn_out = dr("w_ffn_out", [FF, 1024])
    norm1_g = dr("norm1_g", [1024]); norm2_g = dr("norm2_g", [1024]); norm_f_g = dr("norm_f_g", [1024])
    m_norm_g = dr("m_norm_g", [1024]); s_norm_g = dr("s_norm_g", [1024])
    gate_b = dr("gate_b", [8]); dt_bias = dr("dt_bias", [16]); a_log = dr("a_log", [16]); s_d = dr("s_d", [16])
    convw = dr("convw", [2048, 4]); convb = dr("convb", [1, 2048])
    c_ident = dr("c_ident", [128, 128]); c_tri = dr("c_tri", [128, 128]); c_negm = dr("c_negm", [128, 512])
    c_sel = dr("c_sel", [96, 2048]); c_valid = dr("c_valid", [128, 1])
    out = dr("out", [2048, 1024], "ExternalOutput")
    dbg = {}
    if debug:
        for n, s in (("d_uT", [128, 8 * T]), ("d_hmT", [128, 8 * 2048]), ("d_ysT", [128, 8 * 2048]),
                     ("d_mgT", [128, 8 * 2048]), ("d_h2", [128, 16 * 1024]), ("d_g", [128, 17 * 4 * 8])):
            dbg[n] = dr(n, s, "ExternalOutput")

    es = ExitStack()
    try:
      with es:
        tk = Trk(nc, es)
        _NC["trk"] = tk
        def maybe_stop(ph):
            if stop == ph:
                tk.barrier()
                raise _Stop()
        op, dma = tk.op, tk.dma

        def sb(stack, name, shape, dt=F32):
            return stack.enter_context(nc.sbuf_tensor(name, shape, dt))

        class Rot:
            def __init__(self, stack, name, shape, dt, n, hold=False):
                self.t = [sb(stack, f"{name}{i}", shape, dt) for i in range(n)]
                self.name, self.i, self.hold, self.held = name, 0, hold, set()
            def next(self):
                i = self.i % len(self.t); self.i += 1
                if self.hold:
                    assert i not in self.held, f"rotating buffer {self.name} reused while still held"
                    self.held.add(i)
                return self.t[i], (self.name, i)
            def release(self, key):
                self.held.discard(key[1])

        PB = [es.enter_context(nc.psum_tensor(f"pb{i}", [128, 512], F32)) for i in range(6)]
        PTB = [es.enter_context(nc.psum_tensor(f"ptb{i}", [128, 1024], BF16)) for i in range(2)]
        st = {"pb": 0, "ptb": 0}
        def pbank():
            i = st["pb"] % 6; st["pb"] += 1
            return PB[i], ("pb", i)
        def ptbank():
            i = st["ptb"] % 2; st["ptb"] += 1
            return PTB[i], ("ptb", i)

        identf = sb(es, "identf", [128, 128]); identb = sb(es, "identb", [128, 128], BF16)
        tri = sb(es, "tri", [128, 128]); ones = sb(es, "ones", [128, 128])
        negm = sb(es, "negm", [128, 512], BF16); sel = sb(es, "sel", [96, 2048], BF16); valid = sb(es, "valid", [128, 1])
        dma("sp", "c0", identf[:], c_ident, w=["identf"])
        dma("sp", "c1", tri[:], c_tri, w=["tri"])
        dma("pool", "g2", sel[:], c_sel, w=["sel"])
        dma("sp", "c3", valid[:], c_valid, w=["valid"])
        dma("pool", "g0", identb[:], c_ident, w=["identb"])
        dma("pool", "g1", negm[:], c_negm, w=["negm"])
        op("dve", lambda e: e.memset(ones[:], 1.0), w=["ones"])

        arena = sb(es, "arena", [128, 49152], BF16)

        def aview(off, shape, dt=BF16, parts=128):
            n = 1
            for d_ in shape[1:]:
                n *= d_
            nb = n * (4 if dt == F32 else 2)
            assert off % 4 == 0 and off + nb <= 98304
            v = arena[0:parts, off // 2:(off + nb) // 2]
            if dt == F32:
                v = v.bitcast(F32)
            if len(shape) == 3:
                v = v.rearrange("p (a b) -> p a b", a=shape[1])
            return v

        class ARot:
            def __init__(self, name, off, shape, dt, n):
                sz = 1
                for d_ in shape[1:]:
                    sz *= d_
                sz = (sz * (4 if dt == F32 else 2) + 31) // 32 * 32
                self.t = [aview(off + i * sz, shape, dt) for i in range(n)]
                self.name, self.i, self.end = name, 0, off + n * sz
            def next(self):
                i = self.i % len(self.t); self.i += 1
                return self.t[i], (self.name, i)

        R0, R1, R2 = 0, 32768, 65536
        hmT = aview(R0, [128, 8, 2048]); ysT = aview(R1, [128, 8, 2048]); mgT = aview(R2, [128, 8, 2048])
        h2 = aview(R0, [128, 16, 1024], F32); u2T = aview(R2, [128, 8, 2048])
        su = ExitStack()
        su.__enter__()
        uT = sb(su, "uT", [128, 8, T], BF16)

        def load_bc(stack, name, src, n, slot):
            t = sb(stack, name, [128, n])
            dma("sp", slot, t[:], src.partition_broadcast(128), w=[name])
            return t

        def rms_rstd(stack_rot, src_ap, src_keys, n, scale_ap=None, eps=EPS):
            junk, kj = stack_rot["junk"].next()
            ssq, ks = stack_rot["ssq"].next()
            rs, kr = stack_rot["rstd"].next()
            if scale_ap is None:
                op("act", lambda e: e.activation(out=junk[:, 0:n], in_=src_ap, func=AF.Square, accum_out=ssq[:]),
                   r=src_keys, w=[kj, ks])
            else:
                sa, sk = scale_ap
                op("act", lambda e: e.activation(out=junk[:, 0:n], in_=src_ap, func=AF.Square, scale=sa, accum_out=ssq[:]),
                   r=list(src_keys) + [sk], w=[kj, ks])
            op("dve", lambda e: e.tensor_scalar(out=rs[:], in0=ssq[:], scalar1=1.0 / n, scalar2=eps, op0=ALU.mult, op1=ALU.add),
               r=[ks], w=[kr])
            op("act", lambda e: e.activation(out=rs[:], in_=rs[:], func=AF.Sqrt), r=[kr], w=[kr])
            op("dve", lambda e: e.reciprocal(out=rs[:], in_=rs[:]), r=[kr], w=[kr])
            return rs, kr

        def norm_to_T(rots, src_ap, src_keys, g_bc, g_key, dstT, c_dst, dkeys):
            rs, kr = rms_rstd(rots, src_ap, src_keys, 1024)
            yield
            ub, ku = rots["ub"].next()
            op("dve", lambda e: e.scalar_tensor_tensor(out=ub[:], in0=src_ap, scalar=rs[:, 0:1], in1=g_bc[:],
                                                      op0=ALU.mult, op1=ALU.mult), r=list(src_keys) + [kr, g_key], w=[ku])
            yield
            pt, kp = ptbank()
            for kc in range(8):
                op("pe", lambda e: e.transpose(out=pt[:, kc * 128:(kc + 1) * 128], in_=ub[:, kc * 128:(kc + 1) * 128],
                                               identity=identb[:]), r=[ku, "identb"], w=[kp], inc=(kc == 7))
            op("act", lambda e: e.copy(out=dstT[:, :, c_dst * 128:(c_dst + 1) * 128],
                                       in_=pt[:].rearrange("p (k t) -> p k t", k=8)), r=[kp], w=dkeys)

        def run_jobs(jobs, depth, limits=None):
            active, done, nxt, bg = [], set(), 0, []
            limits = limits or {}
            while nxt < len(jobs) or active or bg:
                started = 0
                while nxt < len(jobs) and started < 1:
                    fn, deps = jobs[nxt][0], jobs[nxt][1]
                    cls = jobs[nxt][2] if len(jobs[nxt]) > 2 else "chunk"
                    if deps == "bg":
                        bg.append((nxt, fn(), "bg"))
                        nxt += 1
                        continue
                    ncls = sum(1 for it in active if it[2] == cls)
                    if len(active) >= depth or ncls >= limits.get(cls, depth):
                        break
                    if deps == "drain":
                        if active:
                            break
                    elif not all(d_ in done for d_ in deps):
                        break
                    active.append((nxt, fn(), cls))
                    nxt += 1
                    started += 1
                assert active or bg
                order = [(active, it) for it in active if it[2] == "chunk"] + [(active, it) for it in active if it[2] != "chunk"] \
                    + [(bg, it) for it in bg]
                for lst, item in order:
                    try:
                        next(item[1])
                    except StopIteration:
                        lst.remove(item)
                        done.add(item[0])

        esA = ExitStack()
        with esA:
            g1 = load_bc(esA, "g1bc", norm1_g, 1024, "c0")
            rots = {"junk": Rot(esA, "junk", [128, 1024], BF16, 2), "ssq": Rot(esA, "ssq", [128, 1], F32, 8),
                    "rstd": Rot(esA, "rstd", [128, 1], F32, 8), "ub": Rot(esA, "ub", [128, 1024], BF16, 6)}
            xin = Rot(esA, "xin", [128, 1024], F32, 6)

            def a_job(c):
                def gen():
                    xt, kx = xin.next()
                    dma("sp", f"x{c % 6}", xt[:], h0 if c == 0 else x[(c - 1) * 128:c * 128, :], w=[kx])
                    yield from norm_to_T(rots, xt[:], [kx], g1, "g1bc", uT, c, [("uT", c)])
                return gen
            run_jobs([(a_job(c), ()) for c in range(NCH)], 6)
            tk.barrier()
        maybe_stop("A")
        if debug:
            dma("pool", "dbgp", dbg["d_uT"], uT[:].rearrange("p k t -> p (k t)"), r=[("uT", c) for c in range(NCH)])

        uT_keys = lambda c0, c1: [("uT", c) for c in range(c0, c1)]
        TILES = [(0, 4), (4, 8), (8, 12), (12, 16), (16, 17)]

        def wload(t_ap, src_ap, key, slot):
            dma("pool", slot, t_ap, src_ap, w=[key])

        def wview(src, c0, n):
            return src[:, c0:c0 + n].rearrange("(kc p) n -> p kc n", p=128)

        gb8 = load_bc(su, "gb8", gate_b, 8, "c1")
        wg = sb(su, "wg", [128, 8, 8], BF16)
        wload(wg[:], wview(w_in, OI, 8), "wg", "g0")
        _go = [R2 + 25600]
        def gtile(shape):
            n_ = 4
            for d_ in shape[1:]:
                n_ *= d_
            v = aview(_go[0], shape, F32)
            _go[0] += (n_ + 31) // 32 * 32
            return v
        pre = gtile([128, NCH, 8]); th = gtile([128, NCH, 8])
        ilog, flog, bt, btot, a_, Ab, g_, m_, ea, sc, dn = [gtile([128, NCH, 4]) for _ in range(11)]
        mm = gtile([128, NCH + 1, 4]); amax = gtile([128, 1]); dg = gtile([128, 68])
        assert _go[0] <= R2 + 32768

        def gates_gen():
                pg, kpg = pbank()
                for c in range(NCH):
                    for kc in range(8):
                        op("pe", lambda e: e.matmul(pg[:, c * 8:(c + 1) * 8], lhsT=uT[:, kc, c * 128:(c + 1) * 128], rhs=wg[:, kc, :],
                                                    start=(kc == 0), stop=(kc == 7)), r=[("uT", c), "wg"], w=[kpg],
                           inc=(c == NCH - 1 and kc == 7))
                op("dve", lambda e: e.tensor_tensor(out=pre[:], in0=pg[:, 0:NCH * 8].rearrange("p (c e) -> p c e", e=8),
                                                    in1=gb8[:].unsqueeze(1).to_broadcast([128, NCH, 8]), op=ALU.add),
                   r=[kpg, "gb8"], w=["pre"])
                yield
                op("act", lambda e: e.activation(out=th[:], in_=pre[:], func=AF.Tanh, scale=1.0 / 15.0), r=["pre"], w=["th"])
                yield
                op("dve", lambda e: e.tensor_scalar(out=ilog[:], in0=th[:, :, 0:4], scalar1=15.0, scalar2=None, op0=ALU.mult),
                   r=["th"], w=["ilog"])
                yield
                op("act", lambda e: e.activation(out=flog[:], in_=th[:, :, 4:8], func=AF.Exp, scale=-15.0), r=["th"], w=["flog"])
                yield
                op("act", lambda e: e.activation(out=flog[:], in_=flog[:], func=AF.Ln, bias=1.0), r=["flog"], w=["flog"])
                yield
                op("dve", lambda e: e.tensor_scalar(out=flog[:], in0=flog[:], scalar1=-1.0, scalar2=None, op0=ALU.mult),
                   r=["flog"], w=["flog"])
                yield
                op("dve", lambda e: e.tensor_scalar(out=flog[:, 0, :], in0=flog[:, 0, :], scalar1=valid[:, 0:1], scalar2=None,
                                                    op0=ALU.mult), r=["flog", "valid"], w=["flog"])
                yield
                fl2 = flog[:].rearrange("p c h -> p (c h)")
                p1, k1 = pbank()
                op("pe", lambda e: e.matmul(p1[:, 0:68], lhsT=tri[:], rhs=fl2, start=True, stop=True), r=["tri", "flog"], w=[k1])
                op("act", lambda e: e.copy(out=bt[:].rearrange("p c h -> p (c h)"), in_=p1[:, 0:68]), r=[k1], w=["bt"])
                yield
                p2, k2 = pbank()
                op("pe", lambda e: e.matmul(p2[:, 0:68], lhsT=ones[:], rhs=fl2, start=True, stop=True), r=["ones", "flog"], w=[k2])
                op("act", lambda e: e.copy(out=btot[:].rearrange("p c h -> p (c h)"), in_=p2[:, 0:68]), r=[k2], w=["btot"])
                yield
                op("dve", lambda e: e.tensor_tensor(out=a_[:], in0=ilog[:], in1=bt[:], op=ALU.subtract), r=["ilog", "bt"], w=["a_"])
                yield
                p3, k3 = pbank()
                op("pe", lambda e: e.transpose(out=p3[0:68, 0:128], in_=a_[:].rearrange("p c h -> p (c h)"), identity=identf[:]),
                   r=["a_", "identf"], w=[k3])
                op("dve", lambda e: e.reduce_max(out=amax[0:68, :], in_=p3[0:68, 0:128], axis=AX.X), r=[k3], w=["amax"])
                yield
                op("dve", lambda e: e.tensor_scalar(out=dg[0:68, :], in0=identf[0:68, 0:68], scalar1=amax[0:68, 0:1], scalar2=None,
                                                    op0=ALU.mult), r=["amax", "identf"], w=["dg"])
                yield
                p4, k4 = pbank()
                op("pe", lambda e: e.matmul(p4[:, 0:68], lhsT=ones[0:68, :], rhs=dg[0:68, :], start=True, stop=True),
                   r=["ones", "dg"], w=[k4])
                op("act", lambda e: e.copy(out=Ab[:].rearrange("p c h -> p (c h)"), in_=p4[:, 0:68]), r=[k4], w=["Ab"])
                yield
                op("dve", lambda e: e.memset(mm[:, 0, :], 0.0), w=["mm"])
                yield
                for c in range(NCH):
                    op("dve", lambda e: e.tensor_tensor(out=g_[:, c, :], in0=mm[:, c, :], in1=Ab[:, c, :], op=ALU.max),
                       r=["mm", "Ab"], w=["g_"])
                    op("dve", lambda e: e.tensor_tensor(out=mm[:, c + 1, :], in0=g_[:, c, :], in1=btot[:, c, :], op=ALU.add),
                       r=["g_", "btot"], w=["mm"])
                    yield
                op("dve", lambda e: e.tensor_tensor(out=ea[:], in0=a_[:], in1=g_[:], op=ALU.subtract), r=["a_", "g_"], w=["ea"])
                yield
                op("act", lambda e: e.activation(out=ea[:], in_=ea[:], func=AF.Exp), r=["ea"], w=["ea"])
                yield
                op("dve", lambda e: e.tensor_tensor(out=sc[:], in0=mm[:, 0:NCH, :], in1=g_[:], op=ALU.subtract), r=["mm", "g_"], w=["sc"])
                yield
                op("act", lambda e: e.activation(out=sc[:], in_=sc[:], func=AF.Exp), r=["sc"], w=["sc"])
                yield
                op("dve", lambda e: e.tensor_tensor(out=dn[:], in0=bt[:], in1=g_[:], op=ALU.add), r=["bt", "g_"], w=["dn"])
                yield
                op("act", lambda e: e.activation(out=dn[:], in_=dn[:], func=AF.Exp, scale=-1.0), r=["dn"], w=["dn"])
                yield


        esC = ExitStack()
        with esC:
            sn = load_bc(esC, "snbc", s_norm_g, 1024, "c0")
            dtb = load_bc(esC, "dtb", dt_bias, 16, "c1")
            alg = load_bc(esC, "alg", a_log, 16, "c2")
            Dbc = load_bc(esC, "Dbc", s_d, 16, "c3")
            cw = sb(esC, "cw", [128, 16, 4]); cbrow = sb(esC, "cbrow", [1, 2048], BF16)
            onesb = sb(esC, "onesb", [1, 512], BF16)
            dma("sp", "c0", cw[:], convw.rearrange("(b p) j -> p b j", p=128), w=["cw"])
            dma("pool", "g1", cbrow[:], convb, w=["cbrow"])
            op("dve", lambda e: e.tensor_scalar(out=cbrow[:], in0=cbrow[:], scalar1=0.5, scalar2=None, op0=ALU.mult), r=["cbrow"], w=["cbrow"])
            op("dve", lambda e: e.memset(onesb[:], 1.0), w=["onesb"])
            wdt = sb(esC, "wdt", [128, 8, 16], BF16)
            wload(wdt[:], wview(w_in, ODT, 16), "wdt", "g0")
            pd, kpd = pbank()
            for c in range(NCH):
                for kc in range(8):
                    op("pe", lambda e: e.matmul(pd[:, c * 16:(c + 1) * 16], lhsT=uT[:, kc, c * 128:(c + 1) * 128], rhs=wdt[:, kc, :],
                                                start=(kc == 0), stop=(kc == 7)), r=[("uT", c), "wdt"], w=[kpd],
                       inc=(c == NCH - 1 and kc == 7))
            esCt = ExitStack(); esCt.__enter__()
            H = lambda nm, st_=None: sb(st_ if st_ is not None else esC, nm, [128, NCH, 16])
            dt_, ecum, chdec, dtw = [H(n) for n in ("dt_", "ecum", "chdec", "dtw")]
            Abc = sb(esC, "Abc", [128, 16]); DIm = aview(R2 + 20736, [128, 16, 128])
            xd, ax, dA, cum, ctot, endw = [H(n, esCt) for n in ("xd", "ax", "dA", "cum", "ctot", "endw")]
            bc16 = lambda t: t[:].unsqueeze(1).to_broadcast([128, NCH, 16])
            op("dve", lambda e: e.tensor_tensor(out=xd[:], in0=pd[:, 0:NCH * 16].rearrange("p (c e) -> p c e", e=16),
                                                in1=bc16(dtb), op=ALU.add), r=[kpd, "dtb"], w=["xd"])
            op("act", lambda e: e.activation(out=ax[:], in_=xd[:], func=AF.Abs), r=["xd"], w=["ax"])
            op("act", lambda e: e.activation(out=ax[:], in_=ax[:], func=AF.Exp, scale=-1.0), r=["ax"], w=["ax"])
            op("act", lambda e: e.activation(out=ax[:], in_=ax[:], func=AF.Ln, bias=1.0), r=["ax"], w=["ax"])
            op("dve", lambda e: e.scalar_tensor_tensor(out=dt_[:], in0=xd[:], scalar=0.0, in1=ax[:], op0=ALU.max, op1=ALU.add),
               r=["xd", "ax"], w=["dt_"])
            op("dve", lambda e: e.tensor_scalar(out=dt_[:, 0, :], in0=dt_[:, 0, :], scalar1=valid[:, 0:1], scalar2=None, op0=ALU.mult),
               r=["dt_", "valid"], w=["dt_"])
            op("act", lambda e: e.activation(out=Abc[:], in_=alg[:], func=AF.Exp), r=["alg"], w=["Abc"])
            op("dve", lambda e: e.scalar_tensor_tensor(out=dA[:], in0=dt_[:], scalar=-1.0, in1=bc16(Abc), op0=ALU.mult, op1=ALU.mult),
               r=["dt_", "Abc"], w=["dA"])
            dA2 = dA[:].rearrange("p c h -> p (c h)")
            p1, k1 = pbank()
            op("pe", lambda e: e.matmul(p1[:, 0:272], lhsT=tri[:], rhs=dA2, start=True, stop=True), r=["tri", "dA"], w=[k1])
            op("act", lambda e: e.copy(out=cum[:].rearrange("p c h -> p (c h)"), in_=p1[:, 0:272]), r=[k1], w=["cum"])
            p2, k2 = pbank()
            op("pe", lambda e: e.matmul(p2[:, 0:272], lhsT=ones[:], rhs=dA2, start=True, stop=True), r=["ones", "dA"], w=[k2])
            op("act", lambda e: e.copy(out=ctot[:].rearrange("p c h -> p (c h)"), in_=p2[:, 0:272]), r=[k2], w=["ctot"])
            op("act", lambda e: e.activation(out=ecum[:], in_=cum[:], func=AF.Exp), r=["cum"], w=["ecum"])
            op("act", lambda e: e.activation(out=chdec[:], in_=ctot[:], func=AF.Exp), r=["ctot"], w=["chdec"])
            op("dve", lambda e: e.tensor_tensor(out=endw[:], in0=ctot[:], in1=cum[:], op=ALU.subtract), r=["ctot", "cum"], w=["endw"])
            op("act", lambda e: e.activation(out=endw[:], in_=endw[:], func=AF.Exp), r=["endw"], w=["endw"])
            op("dve", lambda e: e.tensor_tensor(out=dtw[:], in0=endw[:], in1=dt_[:], op=ALU.mult), r=["endw", "dt_"], w=["dtw"])
            maybe_stop("C00")
            cfm = aview(R0 + 17408, [128, T], BF16); ncfm = aview(R2, [128, T], BF16)
            csp = sb(esCt, "csp", [128, NCH, 128], BF16); r1 = sb(esCt, "r1", [128, NCH, 16]); ncl = sb(esCt, "ncl", [128, NCH, 16])
            op("dve", lambda e: e.tensor_scalar(out=ncl[:], in0=dt_[:], scalar1=1e-30, scalar2=None, op0=ALU.max), r=["dt_"], w=["ncl"])
            op("act", lambda e: e.activation(out=ncl[:], in_=ncl[:], func=AF.Ln), r=["ncl"], w=["ncl"])
            op("dve", lambda e: e.tensor_tensor(out=ncl[:], in0=ncl[:], in1=cum[:], op=ALU.subtract), r=["ncl", "cum"], w=["ncl"])
            for src_, srck, dst_, dstk in ((cum, "cum", cfm, "cfm"), (ncl, "ncl", ncfm, "ncfm")):
                op("pool", lambda e: e.memset(csp[:], 0.0), w=["csp"])
                op("dve", lambda e: e.tensor_copy(out=csp[:, :, 0:16], in_=src_[:]), r=[srck], w=["csp"])
                op("dve", lambda e: e.tensor_tensor(out=r1[:], in0=src_[:], in1=csp[:, :, 0:16], op=ALU.subtract), r=[srck, "csp"], w=["r1"])
                op("dve", lambda e: e.tensor_copy(out=csp[:, :, 32:48], in_=r1[:]), r=["r1"], w=["csp"])
                op("dve", lambda e: e.tensor_tensor(out=r1[:], in0=r1[:], in1=csp[:, :, 32:48], op=ALU.subtract), r=["r1", "csp"], w=["r1"])
                op("dve", lambda e: e.tensor_copy(out=csp[:, :, 64:80], in_=r1[:]), r=["r1"], w=["csp"])
                for c0 in range(0, NCH, 8):
                    c1 = min(NCH, c0 + 8)
                    pp, kp = ptbank()
                    for c in range(c0, c1):
                        op("pe", lambda e: e.transpose(out=pp[:, (c - c0) * 128:(c - c0 + 1) * 128], in_=csp[:, c, :], identity=identb[:]),
                           r=["csp", "identb"], w=[kp], inc=(c == c1 - 1))
                    n = (c1 - c0) * 128
                    op("act", lambda e: e.copy(out=dst_[:, c0 * 128:c1 * 128], in_=pp[:, 0:n]), r=[kp], w=[dstk])
            maybe_stop("C01")
            op("dve", lambda e: e.tensor_scalar(out=cw[:], in0=cw[:], scalar1=0.5, scalar2=None, op0=ALU.mult), r=["cw"], w=["cw"])
            for hd in range(16):
                op("dve", lambda e: e.tensor_scalar(out=DIm[:, hd, :], in0=identf[:], scalar1=Dbc[:, hd:hd + 1], scalar2=None, op0=ALU.mult),
                   r=["identf", "Dbc"], w=[("DIm", hd)])
            tk.barrier()
            esCt.__exit__(None, None, None)
            maybe_stop("C0")

            wsf = ARot("wsf", R2 + 4352, [128, 8, 512], BF16, 2)
            wsz = Rot(esC, "wsz", [128, 8, 256], BF16, 2)
            raw = ARot("raw", R0 + 17408 + 4352, [128, 4 + 512], BF16, 2)
            cth = Rot(esC, "cth", [128, 512], BF16, 2); dgms = ARot("dgms", R0 + 24576, [128, 16, 128], BF16, 2)
            fT = aview(R0, [128, 4, T])
            Sb = Rot(esC, "Sb", [128, 256], BF16, 4)
            xtk = Rot(esC, "xtk", [128, 256], BF16, 4); xw = Rot(esC, "xw", [128, 256], BF16, 4)
            btk = Rot(esC, "btk", [128, 128], BF16, 4); cbs = Rot(esC, "cbs", [128, 128], F32, 4)
            dec = Rot(esC, "dec", [128, 512], F32, 4); w4 = Rot(esC, "w4", [128, 512], BF16, 4)
            sz = Rot(esC, "sz", [128, 256], F32, 4); yt = Rot(esC, "yt", [128, 256], F32, 12, hold=True); ysb = Rot(esC, "ysb", [128, 256], BF16, 8)
            rotsC = {"junk": Rot(esC, "junkC", [128, 256], BF16, 2)}

            def load_group(g):
                wf, kf = wsf.next(); wz, kz = wsz.next()
                wload(wf[:, :, 0:256], wview(w_in, OXBC + g * 256, 256), kf, f"wq{g % 2}")
                wload(wf[:, :, 256:384], wview(w_in, OXBC + 1024 + g * 128, 128), kf, f"wk{g % 2}")
                wload(wf[:, :, 384:512], wview(w_in, OXBC + 1536 + g * 128, 128), kf, f"wv{g % 2}")
                wload(wz[:], wview(w_in, OZ + g * 256, 256), kz, f"wo{g % 2}")
                return wf, kf, wz, kz

            carry = sb(esC, "carry", [128, 4, 4], BF16)
            ssq4 = Rot(esC, "ssq4", [128, 4], F32, 4)
            tbuf, ybuf = {}, {}
            tile_of = {c: k for k, (a_, b_) in enumerate(TILES) for c in range(a_, b_)}
            Srot = Rot(esC, "Srot", [128, 256], F32, 2)
            grp = {}

            def conv_job(g, k):
                def gen():
                    wf, kf = grp[g][0], grp[g][1]
                    c0, c1 = TILES[k]
                    n = (c1 - c0) * 128
                    blkch = [g * 2, g * 2 + 1, 8 + g, 12 + g]
                    for b in range(4):
                        cb_i = blkch[b]
                        pp, kp = pbank()
                        for kc in range(8):
                            op("pe", lambda e: e.matmul(pp[:, 0:n], lhsT=wf[:, kc, b * 128:(b + 1) * 128], rhs=uT[:, kc, c0 * 128:c1 * 128],
                                                        start=(kc == 0), stop=(kc == 7)), r=[kf] + uT_keys(c0, c1), w=[kp], inc=(kc == 7))
                        rw, krw = raw.next()
                        if k == 0:
                            op("pool", lambda e: e.memset(rw[:, 0:3], 0.0), w=[krw])
                        else:
                            op("pool", lambda e: e.tensor_copy(out=rw[:, 0:3], in_=carry[:, b, 0:3]), r=[("carry", b)], w=[krw])
                        op("act", lambda e: e.copy(out=rw[:, 3:3 + n], in_=pp[:, 0:n]), r=[kp], w=[krw])
                        if k < len(TILES) - 1:
                            op("pool", lambda e: e.tensor_copy(out=carry[:, b, 0:3], in_=rw[:, n:n + 3]), r=[krw], w=[("carry", b)])
                        yield
                        dgm, kdg = grp[g][6], grp[g][7]
                        pc, kpc = pbank()
                        for j in range(4):
                            op("pe", lambda e: e.matmul(pc[:, 0:n], lhsT=dgm[:, b * 4 + j, :], rhs=rw[:, j:j + n], start=(j == 0), stop=False),
                               r=[kdg, krw], w=[kpc], inc=False)
                        op("pe", lambda e: e.matmul(pc[:, 0:n], lhsT=cbrow[0:1, cb_i * 128:(cb_i + 1) * 128], rhs=onesb[0:1, 0:n],
                                                    start=False, stop=True), r=["cbrow", "onesb"], w=[kpc])
                        th_, kth = cth.next()
                        op("act", lambda e: e.activation(out=th_[:, 0:n], in_=pc[:, 0:n], func=AF.Tanh), r=[kpc], w=[kth])
                        op("dve", lambda e: e.scalar_tensor_tensor(out=fT[:, b, c0 * 128:c1 * 128], in0=th_[:, 0:n], scalar=1.0, in1=pc[:, 0:n],
                                                                  op0=ALU.add, op1=ALU.mult), r=[kth, kpc],
                           w=[("fT", b, c) for c in range(c0, c1)])
                        yield
                return gen

            def chunk_job(g, c):
                def gen():
                    wz, kz = grp[g][2], grp[g][3]
                    S, kS = grp[g][4], grp[g][5]
                    cs = slice(c * 128, (c + 1) * 128)
                    hs = slice(4 * g, 4 * g + 4)
                    b4 = lambda t: t[:, c, hs].unsqueeze(2).to_broadcast([128, 4, 64])
                    v3 = lambda ap: ap.rearrange("p (j d) -> p j d", j=4)
                    pt, kp = ptbank()
                    for j in range(2):
                        op("pe", lambda e: e.transpose(out=pt[:, j * 128:(j + 1) * 128], in_=fT[:, j, cs], identity=identb[:]),
                           r=[("fT", j, c), "identb"], w=[kp], inc=False)
                    op("pe", lambda e: e.transpose(out=pt[:, 256:384], in_=fT[:, 2, cs], identity=identb[:]),
                       r=[("fT", 2, c), "identb"], w=[kp])
                    xt_, kxt = xtk.next(); xw_, kxw = xw.next(); bk, kbk = btk.next()
                    op("act", lambda e: e.copy(out=xt_[:], in_=pt[:, 0:256]), r=[kp], w=[kxt])
                    op("act", lambda e: e.copy(out=bk[:], in_=pt[:, 256:384]), r=[kp], w=[kbk])
                    op("pool", lambda e: e.tensor_tensor(out=v3(xw_[:]), in0=v3(xt_[:]), in1=b4(dtw), op=ALU.mult), r=[kxt, "dtw"], w=[kxw])
                    yield
                    if c >= 1:
                        pz, kpz = pbank()
                        for kc in range(8):
                            op("pe", lambda e: e.matmul(pz[:, 0:256], lhsT=uT[:, kc, cs], rhs=wz[:, kc, :], start=(kc == 0), stop=(kc == 7)),
                               r=[kz, ("uT", c)], w=[kpz], inc=(kc == 7))
                        sz_, ksz = sz.next()
                        op("act", lambda e: e.activation(out=sz_[:], in_=pz[:, 0:256], func=AF.Tanh, scale=0.5), r=[kpz], w=[ksz])
                        op("dve", lambda e: e.scalar_tensor_tensor(out=sz_[:], in0=sz_[:], scalar=1.0, in1=pz[:, 0:256],
                                                                  op0=ALU.add, op1=ALU.mult), r=[ksz, kpz], w=[ksz])
                        pcb, kpcb = pbank()
                        op("pe", lambda e: e.matmul(pcb[:, 0:128], lhsT=fT[:, 2, cs], rhs=fT[:, 3, cs], start=True, stop=True),
                           r=[("fT", 2, c), ("fT", 3, c)], w=[kpcb])
                        cb_, kcb = cbs.next()
                        op("act", lambda e: e.copy(out=cb_[:], in_=pcb[:, 0:128]), r=[kpcb], w=[kcb])
                        pe_, kpe = pbank()
                        op("pe", lambda e: e.matmul(pe_[:], lhsT=ncfm[0:96, cs], rhs=sel[:, 4 * g * 128:(4 * g + 4) * 128], start=True, stop=False),
                           r=["ncfm", "sel"], w=[kpe], inc=False)
                        for j in range(4):
                            hd = 4 * g + j
                            op("pe", lambda e: e.matmul(pe_[:, j * 128:(j + 1) * 128], lhsT=sel[:, hd * 128:(hd + 1) * 128], rhs=cfm[0:96, cs],
                                                        start=False, stop=False), r=["cfm", "sel"], w=[kpe], inc=False)
                        op("pe", lambda e: e.matmul(pe_[:], lhsT=identb[:], rhs=negm[:], start=False, stop=True),
                           r=["identb", "negm"], w=[kpe])
                        dc, kdc = dec.next()
                        op("act", lambda e: e.activation(out=dc[:], in_=pe_[:], func=AF.Exp), r=[kpe], w=[kdc])
                        yield
                    if c >= 1:
                        w4_, kw4 = w4.next()
                        op("dve", lambda e: e.tensor_tensor(out=w4_[:].rearrange("p (j t) -> p j t", j=4),
                                                            in0=dc[:].rearrange("p (j t) -> p j t", j=4),
                                                            in1=cb_[:].unsqueeze(1).to_broadcast([128, 4, 128]), op=ALU.mult),
                           r=[kdc, kcb], w=[kw4])
                        sb_, ksb = Sb.next()
                        op("act", lambda e: e.copy(out=sb_[:], in_=S[:]), r=[kS], w=[ksb])
                        yield
                        po, kpo = pbank()
                        op("pe", lambda e: e.matmul(po[:, 0:256], lhsT=fT[:, 3, cs], rhs=sb_[:], start=True, stop=True),
                           r=[("fT", 3, c), ksb], w=[kpo])
                        y_, ky = yt.next()
                        op("dve", lambda e: e.tensor_tensor(out=v3(y_[:]), in0=v3(po[:, 0:256]), in1=b4(ecum), op=ALU.mult), r=[kpo, "ecum"], w=[ky])
                        py, kpy = pbank()
                        for j in range(4):
                            op("pe", lambda e: e.matmul(py[:, j * 64:(j + 1) * 64], lhsT=w4_[:, j * 128:(j + 1) * 128], rhs=xt_[:, j * 64:(j + 1) * 64],
                                                        start=True, stop=False), r=[kw4, kxt], w=[kpy], inc=False)
                            op("pe", lambda e: e.matmul(py[:, j * 64:(j + 1) * 64], lhsT=DIm[:, 4 * g + j, :], rhs=xt_[:, j * 64:(j + 1) * 64],
                                                        start=False, stop=True), r=[("DIm", 4 * g + j), kxt], w=[kpy], inc=(j == 3))
                        op("dve", lambda e: e.tensor_tensor(out=y_[:], in0=y_[:], in1=py[:, 0:256], op=ALU.add), r=[ky, kpy], w=[ky])
                    if c < NCH - 1:
                        pst, kst = pbank()
                        op("pe", lambda e: e.matmul(pst[:, 0:256], lhsT=bk[:], rhs=xw_[:], start=True, stop=True), r=[kbk, kxw], w=[kst])
                        if c == 0:
                            op("dve", lambda e: e.tensor_copy(out=S[:], in_=pst[:, 0:256]), r=[kst], w=[kS])
                        else:
                            op("pool", lambda e: e.tensor_tensor(out=v3(S[:]), in0=v3(S[:]), in1=b4(chdec), op=ALU.mult), r=[kS, "chdec"], w=[kS])
                            op("dve", lambda e: e.tensor_tensor(out=S[:], in0=S[:], in1=pst[:, 0:256], op=ALU.add), r=[kS, kst], w=[kS])
                    if c == 0:
                        return
                    yield
                    op("dve", lambda e: e.tensor_tensor(out=y_[:], in0=y_[:], in1=sz_[:], op=ALU.mult), r=[ky, ksz], w=[ky])
                    k_ = tile_of[c]
                    if (g, k_) not in tbuf:
                        tbuf[(g, k_)] = ssq4.next()
                    s4, ks4 = tbuf[(g, k_)]
                    cc = c - max(TILES[k_][0], 1)
                    junk, kj = rotsC["junk"].next()
                    op("act", lambda e: e.activation(out=junk[:], in_=y_[:], func=AF.Square, accum_out=s4[:, cc:cc + 1]),
                       r=[ky], w=[kj, (ks4, cc)])
                    ybuf[(g, c)] = (y_, ky)
                return gen

            def back_job(g, k):
                def gen():
                    c0 = max(TILES[k][0], 1); c1 = TILES[k][1]; ncc = c1 - c0
                    s4, ks4 = tbuf[(g, k)]
                    k4 = [(ks4, cc) for cc in range(ncc)]
                    op("dve", lambda e: e.tensor_scalar(out=s4[:, 0:ncc], in0=s4[:, 0:ncc], scalar1=1.0 / 256, scalar2=4 * EPS,
                                                        op0=ALU.mult, op1=ALU.add), r=k4, w=k4)
                    op("act", lambda e: e.activation(out=s4[:, 0:ncc], in_=s4[:, 0:ncc], func=AF.Sqrt), r=k4, w=k4)
                    yield
                    op("dve", lambda e: e.reciprocal(out=s4[:, 0:ncc], in_=s4[:, 0:ncc]), r=k4, w=k4)
                    yield
                    ybs = []
                    for cc in range(ncc):
                        y_, ky = ybuf.pop((g, c0 + cc))
                        yb, kyb = ysb.next()
                        op("dve", lambda e: e.scalar_tensor_tensor(out=yb[:], in0=y_[:], scalar=s4[:, cc:cc + 1], in1=sn[:, g * 256:(g + 1) * 256],
                                                                  op0=ALU.mult, op1=ALU.mult), r=[ky, (ks4, cc), "snbc"], w=[kyb])
                        yt.release(ky)
                        ybs.append((yb, kyb))
                    yield
                    pt2, kp2 = ptbank()
                    last = (ncc - 1, 1)
                    for cc in range(ncc):
                        yb, kyb = ybs[cc]
                        for j in range(2):
                            col = (j * ncc + cc) * 128
                            op("pe", lambda e: e.transpose(out=pt2[:, col:col + 128], in_=yb[:, j * 128:(j + 1) * 128],
                                                           identity=identb[:]), r=[kyb, "identb"], w=[kp2], inc=((cc, j) == last))
                    op("act", lambda e: e.copy(out=ysT[:, 2 * g:2 * g + 2, (c0 - 1) * 128:(c1 - 1) * 128],
                                               in_=pt2[:, 0:2 * ncc * 128].rearrange("p (k t) -> p k t", k=2)), r=[kp2],
                       w=[("ysT", g, c) for c in range(c0, c1)])
                return gen

            def loadg_job(g):
                def gen():
                    S_, kS_ = Srot.next()
                    dgm, kdg = dgms.next()
                    blkch = [g * 2, g * 2 + 1, 8 + g, 12 + g]
                    for b in range(4):
                        for j in range(4):
                            op("dve", lambda e: e.tensor_scalar(out=dgm[:, b * 4 + j, :], in0=identf[:], scalar1=cw[:, blkch[b], j:j + 1],
                                                                scalar2=None, op0=ALU.mult), r=["identf", "cw"], w=[kdg])
                    grp[g] = load_group(g) + (S_, kS_, dgm, kdg)
                    return
                    yield
                return gen

            jobs = []
            J = lambda fn, deps=(), cls='chunk': (jobs.append((fn, deps, cls)), len(jobs) - 1)[1]
            J(loadg_job(0), "drain"); J(loadg_job(1), "drain")
            cv = {}
            fj = []
            cv[(0, 0)] = J(conv_job(0, 0), (), 'conv'); cv[(0, 1)] = J(conv_job(0, 1), (), 'conv')
            J(gates_gen, "bg")
            for g in range(4):
                for k in range(5):
                    c0, c1 = TILES[k]
                    for c in range(c0, c1):
                        if k == 4 and g < 3:
                            cv[(g + 1, 0)] = J(conv_job(g + 1, 0), (), 'conv'); cv[(g + 1, 1)] = J(conv_job(g + 1, 1), (), 'conv')
                        fj.append(J(chunk_job(g, c), (cv[(g, k)],)))
                    if k + 2 < 5:
                        cv[(g, k + 2)] = J(conv_job(g, k + 2), (), 'conv')
                    if k >= 1:
                        kk = k - 1
                        n0 = max(TILES[kk][0], 1)
                        J(back_job(g, kk), tuple(fj[g * NCH + c_] for c_ in range(n0, TILES[kk][1])), 'back')
                J(back_job(g, 4), (fj[g * NCH + 16],), 'back')
                if g + 2 < 4:
                    J(loadg_job(g + 2), "drain")
            run_jobs(jobs, 7, {'chunk': 4, 'conv': 2, 'back': 1})
            tk.barrier()
        maybe_stop("C")
        if debug:
            dma("pool", "dbgp", dbg["d_ysT"], ysT[:].rearrange("p k t -> p (k t)"),
                r=[("ysT", g, c) for g in range(4) for c in range(1, NCH)])

        sD0 = ExitStack(); sD0.__enter__()
        wd = Rot(sD0, "wd", [128, 8, 512], BF16, 1)

        def load_d(d):
            w_, kw = wd.next()
            wload(w_[:, :, 0:128], wview(m_proj, d * 128, 128), kw, f"dq{d % 2}")
            wload(w_[:, :, 128:256], wview(s_proj, d * 128, 128), kw, f"dk{d % 2}")
            wload(w_[:, :, 256:384], wview(w_in, OGA + d * 128, 128), kw, f"dv{d % 2}")
            wload(w_[:, :, 384:512], wview(w_in, OGB + d * 128, 128), kw, f"do{d % 2}")
            return w_, kw
        d_pre = [load_d(0)]

        esB = ExitStack()
        with esB:
            gn = load_bc(esB, "gnbc", m_norm_g, 1024, "c0")
            op("dve", lambda e: e.tensor_scalar(out=gn[:], in0=gn[:], scalar1=0.5, scalar2=None, op0=ALU.mult), r=["gnbc"], w=["gnbc"])
            wfm = Rot(esB, "wfm", [128, 8, 256], BF16, 2)
            wtk = ARot("wtk", R2, [128, 8, 512], BF16, 2)
            qT = aview(wtk.end, [128, T]); kT = aview(wtk.end + 2 * T, [128, T])
            Csb = Rot(esB, "Csb", [128, 257], BF16, 6)
            vaug = Rot(esB, "vaug", [128, 257], BF16, 6); ktok = Rot(esB, "ktok", [128, 128], BF16, 6)
            so = Rot(esB, "so", [128, 256], F32, 14, hold=True); sm = Rot(esB, "sm", [128, 128], BF16, 6)
            hmb = Rot(esB, "hmb", [128, 256], BF16, 5)
            rotsB = {"junk": Rot(esB, "junkB", [128, 256], BF16, 2)}
            for v_t in vaug.t:
                op("dve", lambda e: e.memset(v_t[:, 256:257], 1.0), w=[("vaug", vaug.t.index(v_t))])

            def load_head(h):
                wf, kf = wfm.next(); wt, kt = wtk.next()
                wload(wf[:, :, 0:128], wview(w_in, OQ + h * 128, 128), kf, f"wq{h % 2}")
                wload(wf[:, :, 128:256], wview(w_in, OK_ + h * 128, 128), kf, f"wk{h % 2}")
                wload(wt[:, :, 0:256], wview(w_in, OV + h * 256, 256), kt, f"wv{h % 2}")
                wload(wt[:, :, 256:512], wview(w_in, OO + h * 256, 256), kt, f"wo{h % 2}")
                return (wf, kf, wt, kt)

            CTrot = Rot(esB, "CTrot", [128, 257], F32, 2)
            numS = Rot(esB, "numS", [128, 257], F32, 14, hold=True)
            ssq4B = Rot(esB, "ssq4B", [128, 4], F32, 5); d14 = Rot(esB, "d14", [128, 4], F32, 5)
            tbufB, nbuf = {}, {}
            tile_of = {c: k for k, (a_, b_) in enumerate(TILES) for c in range(a_, b_)}
            hd_ = {}

            def loadh_job(h):
                def gen():
                    C_, kC_ = CTrot.next()
                    hd_[h] = load_head(h) + (C_, kC_)
                    return
                    yield
                return gen

            def qk_job(h, k):
                def gen():
                    wf, kf = hd_[h][0], hd_[h][1]
                    c0, c1 = TILES[k]
                    n = (c1 - c0) * 128
                    for which, dst, nm in ((0, qT, "qT"), (1, kT, "kT")):
                        pp, kp = pbank()
                        for kc in range(8):
                            op("pe", lambda e: e.matmul(pp[:, 0:n], lhsT=wf[:, kc, which * 128:(which + 1) * 128],
                                                        rhs=uT[:, kc, c0 * 128:c1 * 128], start=(kc == 0), stop=(kc == 7)),
                               r=[kf] + uT_keys(c0, c1), w=[kp], inc=(kc == 7))
                        if which == 0:
                            op("act", lambda e: e.activation(out=dst[:, c0 * 128:c1 * 128], in_=pp[:, 0:n], func=AF.Copy,
                                                             scale=float(128 ** -0.5)), r=[kp], w=[(nm, c) for c in range(c0, c1)])
                        else:
                            op("dve", lambda e: e.tensor_copy(out=dst[:, c0 * 128:c1 * 128], in_=pp[:, 0:n]), r=[kp],
                               w=[(nm, c) for c in range(c0, c1)])
                        yield
                return gen

            def mchunk_job(h, c):
                def gen():
                    wt, kt = hd_[h][2], hd_[h][3]
                    CT, kCT = hd_[h][4], hd_[h][5]
                    cs = slice(c * 128, (c + 1) * 128)
                    eacol = ea[:, c, h:h + 1]
                    pvo, kvo = pbank()
                    for kc in range(8):
                        op("pe", lambda e: e.matmul(pvo[:], lhsT=uT[:, kc, cs], rhs=wt[:, kc, 0:512], start=(kc == 0), stop=(kc == 7)),
                           r=[kt, ("uT", c)], w=[kvo], inc=(kc == 7))
                    pk, kpk = ptbank()
                    op("pe", lambda e: e.transpose(out=pk[:, 0:128], in_=kT[:, cs], identity=identb[:]), r=[("kT", c), "identb"], w=[kpk])
                    va, kva = vaug.next(); ktk, kkt = ktok.next()
                    op("act", lambda e: e.copy(out=va[:, 0:256], in_=pvo[:, 0:256]), r=[kvo], w=[kva])
                    op("act", lambda e: e.activation(out=ktk[:], in_=pk[:, 0:128], func=AF.Copy, scale=eacol), r=[kpk, "ea"], w=[kkt])
                    if c >= 1:
                        sot, kso = so.next()
                        op("act", lambda e: e.activation(out=sot[:], in_=pvo[:, 256:512], func=AF.Tanh, scale=0.5), r=[kvo], w=[kso])
                        op("dve", lambda e: e.scalar_tensor_tensor(out=sot[:], in0=sot[:], scalar=1.0, in1=gn[:, h * 256:(h + 1) * 256],
                                                                   op0=ALU.add, op1=ALU.mult), r=[kso, "gnbc"], w=[kso])
                    yield
                    if c >= 1:
                        psc, ksc = pbank()
                        op("pe", lambda e: e.matmul(psc[:, 0:128], lhsT=kT[:, cs], rhs=qT[:, cs], start=True, stop=True),
                           r=[("kT", c), ("qT", c)], w=[ksc])
                        smt, ksm = sm.next()
                        op("dve", lambda e: e.scalar_tensor_tensor(out=smt[:], in0=psc[:, 0:128], scalar=eacol, in1=tri[:],
                                                                  op0=ALU.mult, op1=ALU.mult), r=[ksc, "ea", "tri"], w=[ksm])
                        cb_, kcb = Csb.next()
                        op("act", lambda e: e.activation(out=cb_[:], in_=CT[:], func=AF.Copy, scale=sc[:, c, h:h + 1]),
                           r=[kCT, "sc"], w=[kcb])
                        yield
                        pn, kpn = pbank()
                        op("pe", lambda e: e.matmul(pn[:, 0:257], lhsT=smt[:], rhs=va[:], start=True, stop=False),
                           r=[ksm, kva], w=[kpn], inc=False)
                        op("pe", lambda e: e.matmul(pn[:, 0:257], lhsT=qT[:, cs], rhs=cb_[:], start=False, stop=True),
                           r=[("qT", c), kcb], w=[kpn])
                        nS, knS = numS.next()
                        op("act", lambda e: e.copy(out=nS[:], in_=pn[:, 0:257]), r=[kpn], w=[knS])
                    if c < NCH - 1:
                        pst, kst = pbank()
                        op("pe", lambda e: e.matmul(pst[:, 0:257], lhsT=ktk[:], rhs=va[:], start=True, stop=True),
                           r=[kkt, kva], w=[kst])
                        if c == 0:
                            op("dve", lambda e: e.tensor_copy(out=CT[:], in_=pst[:, 0:257]), r=[kst], w=[kCT])
                        else:
                            op("dve", lambda e: e.scalar_tensor_tensor(out=CT[:], in0=CT[:], scalar=sc[:, c, h:h + 1], in1=pst[:, 0:257],
                                                                      op0=ALU.mult, op1=ALU.add), r=[kst, kCT, "sc"], w=[kCT])
                    if c == 0:
                        return
                    yield
                    k_ = tile_of[c]
                    if (h, k_) not in tbufB:
                        tbufB[(h, k_)] = (ssq4B.next(), d14.next())
                    (s4, ks4), (d4, kd4) = tbufB[(h, k_)]
                    cc = c - max(TILES[k_][0], 1)
                    d1 = d4[:, cc:cc + 1]; kd1 = (kd4, cc)
                    op("act", lambda e: e.activation(out=d1, in_=nS[:, 256:257], func=AF.Abs), r=[knS], w=[kd1])
                    op("dve", lambda e: e.tensor_scalar(out=d1, in0=d1, scalar1=dn[:, c, h:h + 1], scalar2=None,
                                                        op0=ALU.max), r=[kd1, "dn"], w=[kd1])
                    yield
                    op("dve", lambda e: e.reciprocal(out=d1, in_=d1), r=[kd1], w=[kd1])
                    yield
                    junk, kj = rotsB["junk"].next()
                    op("act", lambda e: e.activation(out=junk[:], in_=nS[:, 0:256], func=AF.Square, scale=d1, accum_out=s4[:, cc:cc + 1]),
                       r=[knS, kd1], w=[kj, (ks4, cc)])
                    nbuf[(h, c)] = (nS, knS, sot, kso)
                return gen

            def mback_job(h, k):
                def gen():
                    c0 = max(TILES[k][0], 1); c1 = TILES[k][1]; ncc = c1 - c0
                    (s4, ks4), (d4, kd4) = tbufB[(h, k)]
                    k4 = [(ks4, cc) for cc in range(ncc)]; kd = [(kd4, cc) for cc in range(ncc)]
                    op("dve", lambda e: e.tensor_scalar(out=s4[:, 0:ncc], in0=s4[:, 0:ncc], scalar1=1.0 / 256, scalar2=EPS,
                                                        op0=ALU.mult, op1=ALU.add), r=k4, w=k4)
                    op("act", lambda e: e.activation(out=s4[:, 0:ncc], in_=s4[:, 0:ncc], func=AF.Sqrt), r=k4, w=k4)
                    yield
                    op("dve", lambda e: e.reciprocal(out=s4[:, 0:ncc], in_=s4[:, 0:ncc]), r=k4, w=k4)
                    yield
                    op("dve", lambda e: e.tensor_tensor(out=s4[:, 0:ncc], in0=s4[:, 0:ncc], in1=d4[:, 0:ncc], op=ALU.mult), r=k4 + kd, w=k4)
                    yield
                    hbs = []
                    for cc in range(ncc):
                        nS, knS, sot, kso = nbuf.pop((h, c0 + cc))
                        hb, khb = hmb.next()
                        op("dve", lambda e: e.scalar_tensor_tensor(out=hb[:], in0=nS[:, 0:256], scalar=s4[:, cc:cc + 1], in1=sot[:],
                                                                  op0=ALU.mult, op1=ALU.mult), r=[knS, (ks4, cc), kso], w=[khb])
                        numS.release(knS); so.release(kso)
                        hbs.append((hb, khb))
                    yield
                    pt, kp = ptbank()
                    last = (ncc - 1, 1)
                    for cc in range(ncc):
                        hb, khb = hbs[cc]
                        for j in range(2):
                            col = (j * ncc + cc) * 128
                            op("pe", lambda e: e.transpose(out=pt[:, col:col + 128], in_=hb[:, j * 128:(j + 1) * 128],
                                                           identity=identb[:]), r=[khb, "identb"], w=[kp], inc=((cc, j) == last))
                    op("act", lambda e: e.copy(out=hmT[:, 2 * h:2 * h + 2, (c0 - 1) * 128:(c1 - 1) * 128],
                                               in_=pt[:, 0:2 * ncc * 128].rearrange("p (k t) -> p k t", k=2)), r=[kp],
                       w=[("hmT", h, c) for c in range(c0, c1)])
                return gen

            jobs = []
            J = lambda fn, deps=(), cls='chunk': (jobs.append((fn, deps, cls)), len(jobs) - 1)[1]
            J(loadh_job(0), "drain"); J(loadh_job(1), "drain")
            qj = {}
            fjB = []
            qj[(0, 0)] = J(qk_job(0, 0), (), 'qk'); qj[(0, 1)] = J(qk_job(0, 1), (), 'qk')
            for h in range(4):
                for k in range(5):
                    c0, c1 = TILES[k]
                    for c in range(c0, c1):
                        if k == 4 and h < 3:
                            qj[(h + 1, 0)] = J(qk_job(h + 1, 0), (), 'qk'); qj[(h + 1, 1)] = J(qk_job(h + 1, 1), (), 'qk')
                        fjB.append(J(mchunk_job(h, c), (qj[(h, k)],)))
                    if k + 2 < 5:
                        qj[(h, k + 2)] = J(qk_job(h, k + 2), (), 'qk')
                    if k >= 1:
                        kk = k - 1
                        n0 = max(TILES[kk][0], 1)
                        J(mback_job(h, kk), tuple(fjB[h * NCH + c_] for c_ in range(n0, TILES[kk][1])), 'back')
                J(mback_job(h, 4), (fjB[h * NCH + 16],), 'back')
                if h + 2 < 4:
                    J(loadh_job(h + 2), "drain")
            run_jobs(jobs, 9, {'chunk': 6, 'qk': 2, 'back': 1})
            tk.barrier()
        maybe_stop("B")
        if debug:
            dma("pool", "dbgp", dbg["d_hmT"], hmT[:].rearrange("p k t -> p (k t)"),
                r=[("hmT", h, c) for h in range(4) for c in range(1, NCH)])

        esD = ExitStack()
        with esD:
            sg = Rot(esD, "sg", [128, 512], F32, 4)
            wd.t.append(sb(esD, "wd1b", [128, 8, 512], BF16))
            d_pre.append(load_d(1))
            for d in range(8):
                w_, kw = d_pre[d] if d < 2 else nxt
                if 2 <= d + 1 < 8:
                    nxt = load_d(d + 1)
                for tl in range(4):
                    ts_ = slice(tl * 512, (tl + 1) * 512)
                    tu = slice(128 + tl * 512, 128 + (tl + 1) * 512)
                    hk = [("hmT", h, c) for h in range(4) for c in range(1 + tl * 4, 5 + tl * 4)]
                    yk = [("ysT", g, c) for g in range(4) for c in range(1 + tl * 4, 5 + tl * 4)]
                    uk = uT_keys(1 + tl * 4, 5 + tl * 4)
                    ps_ = []
                    for i, (src, keys) in enumerate(((hmT, hk), (ysT, yk), (uT, uk), (uT, uk))):
                        pp, kp = pbank()
                        sl = ts_ if i < 2 else tu
                        for kc in range(8):
                            op("pe", lambda e: e.matmul(pp[:], lhsT=w_[:, kc, i * 128:(i + 1) * 128], rhs=src[:, kc, sl],
                                                        start=(kc == 0), stop=(kc == 7)), r=[kw] + keys, w=[kp], inc=(kc == 7))
                        ps_.append((pp, kp))
                    sa, ksa = sg.next(); sb2, ksb2 = sg.next()
                    op("act", lambda e: e.activation(out=sa[:], in_=ps_[2][0][:], func=AF.Sigmoid), r=[ps_[2][1]], w=[ksa])
                    op("act", lambda e: e.activation(out=sb2[:], in_=ps_[3][0][:], func=AF.Sigmoid), r=[ps_[3][1]], w=[ksb2])
                    op("dve", lambda e: e.tensor_tensor(out=sa[:], in0=sa[:], in1=ps_[0][0][:], op=ALU.mult), r=[ksa, ps_[0][1]], w=[ksa])
                    op("dve", lambda e: e.tensor_tensor(out=sb2[:], in0=sb2[:], in1=ps_[1][0][:], op=ALU.mult), r=[ksb2, ps_[1][1]], w=[ksb2])
                    op("pool", lambda e: e.tensor_tensor(out=mgT[:, d, ts_], in0=sa[:], in1=sb2[:], op=ALU.add), r=[ksa, ksb2],
                       w=[("mgT", d, c_) for c_ in range(tl * 4, tl * 4 + 4)])
            tk.barrier()
        maybe_stop("D")
        if debug:
            dma("pool", "dbgp", dbg["d_mgT"], mgT[:].rearrange("p k t -> p (k t)"), r=[("mgT", d, c_) for d in range(8) for c_ in range(16)])
        tk.barrier()
        sD0.__exit__(None, None, None)
        su.__exit__(None, None, None)

        sF0 = ExitStack(); sF0.__enter__()
        PASS = [(0, 4), (4, 8), (8, 12), (12, 16), (16, 19), (19, 22)]
        NP = len(PASS)
        wfi = Rot(sF0, "wfi", [128, 8, 2 * 512], BF16, 2)
        wfo = Rot(sF0, "wfo", [128, 4, 1024], BF16, 2)

        def load_pass(p):
            f0, f1 = PASS[p]; nf = f1 - f0
            wi, ki = wfi.next(); wo_, ko = wfo.next()
            wload(wi[:, :, 0:nf * 128], wview(w_ffn_in, f0 * 128, nf * 128), ki, f"fq{p % 2}")
            wload(wi[:, :, 512:512 + nf * 128], wview(w_ffn_in, FF + f0 * 128, nf * 128), ki, f"fk{p % 2}")
            wload(wo_[:, 0:nf, :], w_ffn_out[f0 * 128:f1 * 128, :].rearrange("(f p) n -> p f n", p=128), ko, f"fv{p % 2}")
            return wi, ki, wo_, ko

        esE = ExitStack()
        with esE:
            wo = sb(esE, "wo_", [128, 8, 1024], BF16)
            for i in range(2):
                wload(wo[:, :, i * 512:(i + 1) * 512], wview(w_out, i * 512, 512), ("wo_", i), f"wq{i}")
            p_pre = [load_pass(0), load_pass(1)]
            g2 = load_bc(esE, "g2bc", norm2_g, 1024, "c0")
            rots = {"junk": Rot(esE, "junkF", [128, 1024], BF16, 2), "ssq": Rot(esE, "ssqF", [128, 1], F32, 8),
                    "rstd": Rot(esE, "rstdF", [128, 1], F32, 8), "ub": Rot(esE, "ubF", [128, 1024], BF16, 4)}
            xin = Rot(esE, "xinE", [128, 1024], F32, 4)

            def e_job(c):
                def gen():
                    xt, kx = xin.next()
                    dma("sp", f"x{c % 4}", xt[:], x[c * 128:(c + 1) * 128, :], w=[kx])
                    mk = [("mgT", d, c) for d in range(8)]
                    for hf in range(2):
                        pp, kp = pbank()
                        for kc in range(8):
                            op("pe", lambda e: e.matmul(pp[:], lhsT=mgT[:, kc, c * 128:(c + 1) * 128], rhs=wo[:, kc, hf * 512:(hf + 1) * 512],
                                                        start=(kc == 0), stop=(kc == 7)), r=[("wo_", hf)] + mk, w=[kp], inc=(kc == 7))
                        op("dve", lambda e: e.tensor_tensor(out=h2[:, c, hf * 512:(hf + 1) * 512], in0=pp[:], in1=xt[:, hf * 512:(hf + 1) * 512],
                                                            op=ALU.add), r=[kp, kx], w=[("h2", c, hf)])
                    yield
                    yield from norm_to_T(rots, h2[:, c, :], [("h2", c, 0), ("h2", c, 1)], g2, "g2bc", u2T, c, [("u2T", c)] + mk)
                return gen
            run_jobs([(e_job(c), ()) for c in range(16)], 4)
            tk.barrier()
        maybe_stop("E")
        if debug:
            dma("sp", "dbg", dbg["d_h2"], h2[:].rearrange("p c d -> p (c d)"), r=[("h2", c, hf) for c in range(16) for hf in range(2)])

        esF = ExitStack()
        with esF:
            gf = load_bc(esF, "gfbc", norm_f_g, 1024, "c0")
            rotsG = {"junk": Rot(esF, "junkG", [128, 1024], BF16, 2), "ssq": Rot(esF, "ssqG", [128, 1], F32, 8),
                     "rstd": Rot(esF, "rstdG", [128, 1], F32, 8)}
            ob = Rot(esF, "ob", [128, 1024], F32, 2)

            def g_job(c):
                def gen():
                    rs, kr = rms_rstd(rotsG, h2[:, c, :], [("h2", c, 0), ("h2", c, 1)], 1024)
                    yield
                    o_, ko_ = ob.next()
                    op("dve", lambda e: e.scalar_tensor_tensor(out=o_[:], in0=h2[:, c, :], scalar=rs[:, 0:1], in1=gf[:], op0=ALU.mult, op1=ALU.mult),
                       r=[("h2", c, 0), ("h2", c, 1), kr, "gfbc"], w=[ko_])
                    yield
                    dma("sp", f"o{c % 2}", out[c * 128:(c + 1) * 128, :], o_[:], r=[ko_])
                return gen
            pending = []

            def advance():
                for g_ in list(pending):
                    try:
                        next(g_)
                    except StopIteration:
                        pending.remove(g_)
            actT = Rot(esF, "actT", [128, 4, 512], BF16, 2)
            sgF = Rot(esF, "sgF", [128, 512], F32, 3)
            for p in range(NP):
                f0, f1 = PASS[p]; nf = f1 - f0
                wi, ki, wo_, ko = p_pre[p] if p < 2 else nxt
                if 2 <= p + 1 < NP:
                    nxt = load_pass(p + 1)
                for tl in range(4):
                    tu = slice(tl * 512, (tl + 1) * 512)
                    uk = [("u2T", c_) for c_ in range(tl * 4, tl * 4 + 4)]
                    at, kat = actT.next()
                    for i in range(nf):
                        pg_, kpg_ = pbank()
                        for kc in range(8):
                            op("pe", lambda e: e.matmul(pg_[:], lhsT=wi[:, kc, i * 128:(i + 1) * 128], rhs=u2T[:, kc, tu],
                                                        start=(kc == 0), stop=(kc == 7)), r=[ki] + uk, w=[kpg_], inc=(kc == 7))
                        pu_, kpu_ = pbank()
                        for kc in range(8):
                            op("pe", lambda e: e.matmul(pu_[:], lhsT=wi[:, kc, 512 + i * 128:512 + (i + 1) * 128], rhs=u2T[:, kc, tu],
                                                        start=(kc == 0), stop=(kc == 7)), r=[ki] + uk, w=[kpu_], inc=(kc == 7))
                        s_, ks_ = sgF.next()
                        op("act", lambda e: e.activation(out=s_[:], in_=pg_[:], func=AF.Silu), r=[kpg_], w=[ks_])
                        op("dve", lambda e: e.tensor_tensor(out=at[:, i, :], in0=s_[:], in1=pu_[:], op=ALU.mult), r=[ks_, kpu_], w=[kat])
                    for cc in range(4):
                        c = tl * 4 + cc
                        for hf in range(2):
                            pp, kp = pbank()
                            for i in range(nf):
                                op("pe", lambda e: e.matmul(pp[:], lhsT=at[:, i, cc * 128:(cc + 1) * 128], rhs=wo_[:, i, hf * 512:(hf + 1) * 512],
                                                            start=(i == 0), stop=(i == nf - 1)), r=[kat, ko], w=[kp], inc=(i == nf - 1))
                            op("dve", lambda e: e.tensor_tensor(out=h2[:, c, hf * 512:(hf + 1) * 512], in0=h2[:, c, hf * 512:(hf + 1) * 512],
                                                                in1=pp[:], op=ALU.add), r=[kp, ("h2", c, hf)], w=[("h2", c, hf)])
                        if p == NP - 1:
                            pending.append(g_job(c)())
                        advance()
            while pending:
                advance()
            tk.barrier()
        sF0.__exit__(None, None, None)

    except _Stop:
        pass
    return nc


def host_inputs(x, meta, norm1_g, w_in, m_igate_b, m_fgate_b, m_norm_g, m_proj, s_conv_w, s_conv_b,
                s_dt_bias, s_A_log, s_D, s_norm_g, s_proj, w_out, norm2_g, w_ffn_in, w_ffn_out, norm_f_g):
    f = lambda a: np.ascontiguousarray(np.asarray(a, dtype=np.float32))
    h0 = np.zeros((128, 1024), np.float32); h0[112:] = f(meta)
    tri = np.triu(np.ones((128, 128), np.float32))
    negm = np.tile(np.where(tri > 0, 0.0, NEG).astype(np.float32), (1, 4))
    sel = np.zeros((96, 16, 128), np.float32)
    for k in range(16):
        for r_ in range(3):
            sel[32 * r_ + k, k, :] = 1.0
    valid = (np.arange(128) >= 112).astype(np.float32).reshape(128, 1)
    shared = {
        "h0": h0, "w_in": f(w_in[0]), "m_proj": f(m_proj[0]), "s_proj": f(s_proj[0]), "w_out": f(w_out[0]),
        "w_ffn_in": f(w_ffn_in[0]), "w_ffn_out": f(w_ffn_out[0]),
        "norm1_g": f(norm1_g[0]), "norm2_g": f(norm2_g[0]), "norm_f_g": f(norm_f_g),
        "m_norm_g": f(m_norm_g[0]).reshape(1024), "s_norm_g": f(s_norm_g[0]),
        "gate_b": np.concatenate([f(m_igate_b[0]), f(m_fgate_b[0])]), "dt_bias": f(s_dt_bias[0]), "a_log": f(s_A_log[0]),
        "s_d": f(s_D[0]), "convw": f(f(s_conv_w[0]).T), "convb": f(f(s_conv_b[0]).reshape(1, 2048)),
        "c_ident": np.eye(128, dtype=np.float32), "c_tri": tri, "c_negm": negm, "c_sel": sel.reshape(96, 2048), "c_valid": valid,
    }
    xs = f(x)
    return [dict(shared, x=xs[b]) for b in range(8)]


def kernel(**inputs):
    in_maps = host_inputs(**inputs)
    if "nc" not in _NC:
        _NC["nc"] = build(False)
    res = run_bass_kernel_spmd(_NC["nc"], in_maps, core_ids=list(range(8)))
    return np.stack([np.asarray(r["out"], dtype=np.float32).reshape(2048, 1024) for r in res.results], axis=0)
```
